# Optimizing a Trainium2 kernel written in Bass

```python
import jax, jax.numpy as jnp
from jax import lax
import numpy as np

D_MODEL = 1024
BATCH = 16
SEQ = 256
DEPTH = 2
DEC_BATCH = 4
DEC_SEQ = 4096
PAST_LEN = 256

GRID_W = 64
D_MIX = D_MODEL
HEAD_DIM = 64
A_HEADS = D_MIX // 128
A_KV_HEADS = A_HEADS // 4
A_GROUP = A_HEADS // A_KV_HEADS
A_WIDTH = A_HEADS * HEAD_DIM
A_KV_WIDTH = A_KV_HEADS * HEAD_DIM
WINDOW = 128
BLOCK = 128
B_WIDTH = D_MIX // 4
B_GROUP_DIM = 64
B_GROUPS = B_WIDTH // B_GROUP_DIM
C_HEADS = D_MIX // 256
C_NOPE = 64
C_ROPE = 32
C_V = 64
C_Q_LORA = 192
C_KV_LORA = 128
C_WIDTH = C_HEADS * C_V
IN_COLS = A_WIDTH + 2 * A_KV_WIDTH + B_WIDTH + C_Q_LORA + C_KV_LORA + C_ROPE
D_FF = 2816
CONV_W = 3
ROPE_BASE = 10000.0
EPS = 1e-6
NEG = -1e30

kernel_name = "hybrid_diffusion_parallel_heads_step"


def _rms_norm(x, g):
    xf = x.astype(jnp.float32)
    y = xf * lax.rsqrt(jnp.mean(xf * xf, axis=-1, keepdims=True) + EPS)
    return (y * g.astype(jnp.float32)).astype(x.dtype)


def _modulation(cvec, w_ada, b_ada):
    m = jax.nn.silu(cvec) @ w_ada + b_ada
    return jnp.split(m[:, None, :], 6, axis=-1)


def _axial_tables(n_tok, dim):
    rows = n_tok // GRID_W
    r = jnp.repeat(jnp.arange(rows), GRID_W).astype(jnp.float32)
    col = jnp.tile(jnp.arange(GRID_W), rows).astype(jnp.float32)
    quarter = dim // 4
    inv = ROPE_BASE ** (-jnp.arange(quarter, dtype=jnp.float32) / quarter)
    ang_r = r[:, None] * inv
    ang_c = col[:, None] * inv
    return (jnp.cos(ang_r), jnp.sin(ang_r), jnp.cos(ang_c), jnp.sin(ang_c))


def _rot(x, cos, sin):
    x1, x2 = jnp.split(x, 2, axis=-1)
    return jnp.concatenate([x1 * cos - x2 * sin, x2 * cos + x1 * sin], axis=-1)


def _apply_axial_rope(x, tables):
    cr, sr, cc, sc = [t.astype(x.dtype)[:, None, :] for t in tables]
    half = x.shape[-1] // 2
    return jnp.concatenate([_rot(x[..., :half], cr, sr), _rot(x[..., half:], cc, sc)], axis=-1)


def _split_proj(p):
    sizes = [A_WIDTH, A_KV_WIDTH, A_KV_WIDTH, B_WIDTH, C_Q_LORA, C_KV_LORA, C_ROPE]
    idx = np.cumsum(sizes)[:-1].tolist()
    return jnp.split(p, idx, axis=-1)


def _sink_softmax(logits, sink_b):
    p = jax.nn.softmax(jnp.concatenate([logits, sink_b], axis=-1), axis=-1)
    return p[..., :-1]


def _ctx_attn_a(q, k, v, sink):
    B, n = q.shape[:2]
    qg = q.reshape(B, n, A_KV_HEADS, A_GROUP, HEAD_DIM)
    s = jnp.einsum('bqhgd,bkhd->bhgqk', qg, k, preferred_element_type=jnp.float32) * (HEAD_DIM ** -0.5)
    sink_b = jnp.broadcast_to(sink.astype(jnp.float32).reshape(1, A_KV_HEADS, A_GROUP, 1, 1), s.shape[:-1] + (1,))
    p = _sink_softmax(s, sink_b)
    out = jnp.einsum('bhgqk,bkhd->bqhgd', p.astype(v.dtype), v)
    return out.reshape(B, n, A_WIDTH)


def _latent_attn_a(q, k, v, k_ctx, v_ctx, sink):
    B, n = q.shape[:2]
    nb = n // BLOCK
    qb = q.reshape(B, nb, BLOCK, A_KV_HEADS, A_GROUP, HEAD_DIM)
    pad = ((0, 0), (BLOCK, BLOCK), (0, 0), (0, 0))
    kp = jnp.pad(k, pad).reshape(B, nb + 2, BLOCK, A_KV_HEADS, HEAD_DIM)
    vp = jnp.pad(v, pad).reshape(B, nb + 2, BLOCK, A_KV_HEADS, HEAD_DIM)
    kband = jnp.concatenate([kp[:, :-2], kp[:, 1:-1], kp[:, 2:]], axis=2)
    vband = jnp.concatenate([vp[:, :-2], vp[:, 1:-1], vp[:, 2:]], axis=2)
    scale = HEAD_DIM ** -0.5
    s_band = jnp.einsum('bnqhgd,bnkhd->bnhgqk', qb, kband, preferred_element_type=jnp.float32) * scale
    qi = jnp.arange(BLOCK)
    kj = jnp.arange(3 * BLOCK)
    rel = qi[:, None] - kj[None, :] + BLOCK
    key_pos = (jnp.arange(nb)[:, None] - 1) * BLOCK + kj[None, :]
    in_range = (key_pos >= 0) & (key_pos < n)
    mask = (jnp.abs(rel) <= WINDOW)[None] & in_range[:, None, :]
    s_band = jnp.where(mask[None, :, None, None], s_band, NEG)
    s_ctx = jnp.einsum('bnqhgd,bchd->bnhgqc', qb, k_ctx, preferred_element_type=jnp.float32) * scale
    sink_b = jnp.broadcast_to(sink.astype(jnp.float32).reshape(1, 1, A_KV_HEADS, A_GROUP, 1, 1), s_band.shape[:-1] + (1,))
    p = _sink_softmax(jnp.concatenate([s_band, s_ctx], axis=-1), sink_b)
    p_band = p[..., :3 * BLOCK].astype(v.dtype)
    p_ctx = p[..., 3 * BLOCK:].astype(v.dtype)
    out = (jnp.einsum('bnhgqk,bnkhd->bnqhgd', p_band, vband)
           + jnp.einsum('bnhgqc,bchd->bnqhgd', p_ctx, v_ctx))
    return out.reshape(B, n, A_WIDTH)


def _fourier_mix(f):
    B, n, _ = f.shape
    fg = f.reshape(B, n, B_GROUPS, B_GROUP_DIM).astype(jnp.float32)
    y = jnp.fft.fft2(fg, axes=(1, 3), norm='ortho').real
    return y.reshape(B, n, B_WIDTH).astype(f.dtype)


def _mla_q(cq, g_cq, w_uq):
    B, n = cq.shape[:2]
    q = (_rms_norm(cq, g_cq) @ w_uq).reshape(B, n, C_HEADS, C_NOPE + C_ROPE)
    return q[..., :C_NOPE], q[..., C_NOPE:]


def _mla_kv(ckv_n, kr, w_ukv):
    B, n = ckv_n.shape[:2]
    kv = (ckv_n @ w_ukv).reshape(B, n, C_HEADS, C_NOPE + C_V)
    k = jnp.concatenate([kv[..., :C_NOPE], jnp.broadcast_to(kr[:, :, None, :], (B, n, C_HEADS, C_ROPE))], axis=-1)
    return k, kv[..., C_NOPE:]


def _ctx_mla(q, k, v):
    B, n = q.shape[:2]
    s = jnp.einsum('bqhd,bkhd->bhqk', q, k, preferred_element_type=jnp.float32) * ((C_NOPE + C_ROPE) ** -0.5)
    p = jax.nn.softmax(s, axis=-1).astype(v.dtype)
    return jnp.einsum('bhqk,bkhd->bqhd', p, v).reshape(B, n, C_WIDTH)


def _latent_mla(q, k, v, k_ctx, v_ctx):
    B, n = q.shape[:2]
    nb = n // BLOCK
    scale = (C_NOPE + C_ROPE) ** -0.5
    qblocks = q.reshape(B, nb, BLOCK, C_HEADS, C_NOPE + C_ROPE).transpose(1, 0, 2, 3, 4)

    def one_block(qb):
        s_lat = jnp.einsum('bqhd,bkhd->bhqk', qb, k, preferred_element_type=jnp.float32)
        s_ctx = jnp.einsum('bqhd,bkhd->bhqk', qb, k_ctx, preferred_element_type=jnp.float32)
        p = jax.nn.softmax(jnp.concatenate([s_lat, s_ctx], axis=-1) * scale, axis=-1).astype(v.dtype)
        return (jnp.einsum('bhqk,bkhd->bqhd', p[..., :n], v)
                + jnp.einsum('bhqk,bkhd->bqhd', p[..., n:], v_ctx))

    out = lax.map(one_block, qblocks)
    return out.transpose(1, 0, 2, 3, 4).reshape(B, n, C_WIDTH)


def _mixer_context(h, w_in, sink, g_cq, w_uq, g_ckv, w_ukv):
    B, n, _ = h.shape
    qa, ka, va, fb, cq, ckv, kr = _split_proj(h @ w_in)
    qa = qa.reshape(B, n, A_HEADS, HEAD_DIM)
    ka = ka.reshape(B, n, A_KV_HEADS, HEAD_DIM)
    va = va.reshape(B, n, A_KV_HEADS, HEAD_DIM)
    out_a = _ctx_attn_a(qa, ka, va, sink)
    out_b = _fourier_mix(fb)
    ckv_n = _rms_norm(ckv, g_ckv)
    q_nope, q_rope = _mla_q(cq, g_cq, w_uq)
    kc, vc = _mla_kv(ckv_n, kr, w_ukv)
    out_c = _ctx_mla(jnp.concatenate([q_nope, q_rope], axis=-1), kc, vc)
    return jnp.concatenate([out_a, out_b, out_c], axis=-1), ka, va, ckv_n, kr


def _mixer_latent(h, k_ctx_a, v_ctx_a, ckv_ctx, kr_ctx, w_in, sink, g_cq, w_uq, g_ckv, w_ukv):
    B, n, _ = h.shape
    qa, ka, va, fb, cq, ckv, kr = _split_proj(h @ w_in)
    tab_a = _axial_tables(n, HEAD_DIM)
    tab_c = _axial_tables(n, C_ROPE)
    qa = _apply_axial_rope(qa.reshape(B, n, A_HEADS, HEAD_DIM), tab_a)
    ka = _apply_axial_rope(ka.reshape(B, n, A_KV_HEADS, HEAD_DIM), tab_a)
    va = va.reshape(B, n, A_KV_HEADS, HEAD_DIM)
    out_a = _latent_attn_a(qa, ka, va, k_ctx_a, v_ctx_a, sink)
    out_b = _fourier_mix(fb)
    ckv_n = _rms_norm(ckv, g_ckv)
    q_nope, q_rope = _mla_q(cq, g_cq, w_uq)
    q_rope = _apply_axial_rope(q_rope, tab_c)
    kr = _apply_axial_rope(kr[:, :, None, :], tab_c)[:, :, 0]
    kc, vc = _mla_kv(ckv_n, kr, w_ukv)
    kc_ctx, vc_ctx = _mla_kv(ckv_ctx, kr_ctx, w_ukv)
    out_c = _latent_mla(jnp.concatenate([q_nope, q_rope], axis=-1), kc, vc, kc_ctx, vc_ctx)
    return jnp.concatenate([out_a, out_b, out_c], axis=-1)


def _conv_ffn(h, w_ug, conv_w, conv_b, w_down):
    n = h.shape[1]
    u = h @ w_ug
    half = CONV_W // 2
    up = jnp.pad(u, ((0, 0), (half, half), (0, 0)))
    uc = sum(up[:, i:i + n] * conv_w[i] for i in range(CONV_W)) + conv_b
    a, g = jnp.split(uc, 2, axis=-1)
    return (jax.nn.silu(g) * a) @ w_down


def setup_inputs(seed: int = 0) -> dict:
    key = jax.random.key(seed)
    ks = jax.random.split(key, 26)
    f32 = jnp.float32

    def nrm(k, shape, scale):
        return jax.random.normal(k, shape, f32) * scale

    return {
        "x_prompt": nrm(ks[0], (BATCH, SEQ, D_MODEL), 1.0),
        "x_sample": nrm(ks[1], (DEC_BATCH, DEC_SEQ, D_MODEL), 1.0),
        "cache_win_k": nrm(ks[2], (DEC_BATCH, DEPTH, PAST_LEN, A_KV_HEADS, HEAD_DIM), 1.0),
        "cache_win_v": nrm(ks[3], (DEC_BATCH, DEPTH, PAST_LEN, A_KV_HEADS, HEAD_DIM), 1.0),
        "cache_mla_ckv": nrm(ks[4], (DEC_BATCH, DEPTH, PAST_LEN, C_KV_LORA), 1.0),
        "cache_mla_krope": nrm(ks[5], (DEC_BATCH, DEPTH, PAST_LEN, C_ROPE), 1.0),
        "c": nrm(ks[6], (DEC_BATCH, D_MODEL), 1.0),
        "c_ctx": nrm(ks[7], (D_MODEL,), 1.0),
        "w_ada": nrm(ks[8], (DEPTH, D_MODEL, 6 * D_MODEL), 0.5 * D_MODEL ** -0.5),
        "b_ada": nrm(ks[9], (DEPTH, 6 * D_MODEL), 0.02),
        "g_mix": 1.0 + nrm(ks[10], (DEPTH, D_MODEL), 0.05),
        "w_in": nrm(ks[11], (DEPTH, D_MODEL, IN_COLS), D_MODEL ** -0.5),
        "sink": nrm(ks[12], (DEPTH, A_HEADS), 0.5),
        "g_cq": 1.0 + nrm(ks[13], (DEPTH, C_Q_LORA), 0.05),
        "w_uq": nrm(ks[14], (DEPTH, C_Q_LORA, C_HEADS * (C_NOPE + C_ROPE)), C_Q_LORA ** -0.5),
        "g_ckv": 1.0 + nrm(ks[15], (DEPTH, C_KV_LORA), 0.05),
        "w_ukv": nrm(ks[16], (DEPTH, C_KV_LORA, C_HEADS * (C_NOPE + C_V)), C_KV_LORA ** -0.5),
        "w_out": nrm(ks[17], (DEPTH, D_MIX, D_MODEL), D_MIX ** -0.5),
        "g_ffn": 1.0 + nrm(ks[18], (DEPTH, D_MODEL), 0.05),
        "w_ug": nrm(ks[19], (DEPTH, D_MODEL, 2 * D_FF), D_MODEL ** -0.5),
        "conv_w": nrm(ks[20], (DEPTH, CONV_W, 2 * D_FF), CONV_W ** -0.5),
        "conv_b": nrm(ks[21], (DEPTH, 2 * D_FF), 0.02),
        "w_down": nrm(ks[22], (DEPTH, D_FF, D_MODEL), D_FF ** -0.5),
        "g_final": 1.0 + nrm(ks[23], (D_MODEL,), 0.05),
    }


def reference(x_prompt, x_sample, cache_win_k, cache_win_v, cache_mla_ckv, cache_mla_krope,
              c, c_ctx, w_ada, b_ada, g_mix, w_in, sink, g_cq, w_uq, g_ckv, w_ukv, w_out,
              g_ffn, w_ug, conv_w, conv_b, w_down, g_final):
    xp = x_prompt
    xs = x_sample
    new_k, new_v, new_ckv, new_kr = [], [], [], []
    for l in range(DEPTH):
        sh1, sc1, gt1, sh2, sc2, gt2 = _modulation(c_ctx[None, :], w_ada[l], b_ada[l])
        h = _rms_norm(xp, g_mix[l]) * (1.0 + sc1) + sh1
        mix, ka, va, ckv_n, kr = _mixer_context(h, w_in[l], sink[l], g_cq[l], w_uq[l], g_ckv[l], w_ukv[l])
        xp = xp + gt1 * (mix @ w_out[l])
        h = _rms_norm(xp, g_ffn[l]) * (1.0 + sc2) + sh2
        xp = xp + gt2 * _conv_ffn(h, w_ug[l], conv_w[l], conv_b[l], w_down[l])
        new_k.append(ka)
        new_v.append(va)
        new_ckv.append(ckv_n)
        new_kr.append(kr)
        sh1, sc1, gt1, sh2, sc2, gt2 = _modulation(c, w_ada[l], b_ada[l])
        h = _rms_norm(xs, g_mix[l]) * (1.0 + sc1) + sh1
        mix = _mixer_latent(h, cache_win_k[:, l], cache_win_v[:, l], cache_mla_ckv[:, l], cache_mla_krope[:, l],
                            w_in[l], sink[l], g_cq[l], w_uq[l], g_ckv[l], w_ukv[l])
        xs = xs + gt1 * (mix @ w_out[l])
        h = _rms_norm(xs, g_ffn[l]) * (1.0 + sc2) + sh2
        xs = xs + gt2 * _conv_ffn(h, w_ug[l], conv_w[l], conv_b[l], w_down[l])
    y_prompt = _rms_norm(xp, g_final)
    y_sample = _rms_norm(xs, g_final)
    state_win_k = jnp.stack(new_k, axis=1)
    state_win_v = jnp.stack(new_v, axis=1)
    state_mla_ckv = jnp.stack(new_ckv, axis=1)
    state_mla_krope = jnp.stack(new_kr, axis=1)
    return (y_prompt, y_sample, state_win_k, state_win_v, state_mla_ckv, state_mla_krope)
```

```python
import os
import numpy as np
import ml_dtypes
import concourse.bass as bass
import concourse.mybir as mybir
from concourse.bass_utils import run_bass_kernel_spmd

F32 = mybir.dt.float32
BF16 = mybir.dt.bfloat16
ALU = mybir.AluOpType
AF = mybir.ActivationFunctionType

L = 2
D = 1024
NS = 4096
NP = 256
NT = NS + 2 * NP
PAST = 256
NK = NS + PAST + 2 * NP
DFF = 2816
NFC = 22
EPS = 1e-6
C_QA, C_QAP = 0, 512
C_KD0, C_KD1, C_KD0P, C_KD1P = 1024, 1152, 1280, 1408
C_KAN, C_VA, C_FB, C_CQ, C_CKV, C_KR, C_KRP = 1536, 1664, 1792, 2048, 2240, 2368, 2464
C1 = 2560

NQH = NS // 2
NQL = NQH + 128


def qtiles(seg, l):
    if seg["pb"] is None and l == L - 1:
        return [(t0, min(512, NQL - t0)) for t0 in range(0, NQL, 512)]
    return [(t0, min(512, seg["n"] - t0)) for t0 in range(0, seg["n"], 512)]


SEGS = [
    dict(name="s", off=0, n=NS, koff=0, nk=NS + PAST, grp=1, rope=True, pb=None),
    dict(name="p0", off=NS, n=NP, koff=NS + PAST, nk=NP, grp=0, rope=False, pb=0),
    dict(name="p1", off=NS + NP, n=NP, koff=NS + PAST + NP, nk=NP, grp=0, rope=False, pb=1),
]


class TU:
    __slots__ = ("w", "r")

    def __init__(self):
        self.w = {}
        self.r = {}


class Buf:
    def __init__(self, t, excl=False):
        self.t = t
        self.tu = TU()
        self.excl = excl

    def __getitem__(self, idx):
        return self.t[idx]


class BankView:
    def __init__(self, t, b, n):
        self.t, self.b, self.n = t, b, n

    def __getitem__(self, idx):
        p, f = idx
        if self.n == 1:
            return self.t[p, self.b, f]
        return self.t[p, self.b:self.b + self.n, f]


class Ring:
    def __init__(self, bufs):
        self.bufs = bufs
        self.i = 0

    def next(self):
        b = self.bufs[self.i]
        self.i = (self.i + 1) % len(self.bufs)
        return b


class Eng:
    def __init__(self, name, sem):
        self.name = name
        self.sem = sem
        self.count = 0
        self.waited = {}
        self.thunks = []
        self.dma_pool = []
        self.dma_next = 0


class FW:
    def __init__(self, nc, n_dma_sems=16):
        self.nc = nc
        self.eng = {}
        for nm in ("pe", "act", "dve", "pool", "sp"):
            self.eng[nm] = Eng(nm, nc.alloc_semaphore(name=f"prog_{nm}"))
        for nm in ("sp", "pool"):
            e = self.eng[nm]
            for i in range(n_dma_sems):
                e.dma_pool.append([nc.alloc_semaphore(name=f"dma_{nm}_{i}"), 0])
        self.n_inst = 0
        self.bg = {}
        self.bg_all = []

    def dma_bg(self, eng, out, in_, tag):
        e = self.eng[eng]
        sem = self.nc.alloc_semaphore(name=f"bg_{len(self.bg_all)}")
        e.thunks.append(lambda h, out=out, in_=in_, sem=sem: h.dma_start(out=out, in_=in_).then_inc(sem, 16))
        self.bg.setdefault(tag, []).append((sem, 16))
        self.bg_all.append((sem, 16))

    def wait_bg(self, eng, tag):
        e = self.eng[eng]
        for sem, val in self.bg.get(tag, []):
            self._wait(e, sem, val)

    @staticmethod
    def _merge(deps, d):
        for s, v in d.items():
            if deps.get(s, (None, 0))[1] < v[1]:
                deps[s] = v

    def _deps(self, reads, writes, own=None):
        deps = {}
        for b in reads:
            self._merge(deps, b.tu.w)
            if b.excl:
                self._merge(deps, {k: v for k, v in b.tu.r.items() if v[0] is not own})
        for b in writes:
            self._merge(deps, b.tu.w)
            self._merge(deps, b.tu.r)
        return deps

    def _wait(self, e, sem, val):
        if e.waited.get(id(sem), 0) < val:
            e.waited[id(sem)] = val
            e.thunks.append(lambda h, sem=sem, val=val: h.wait_ge(sem, val))

    def _emit_waits(self, e, deps, skip_self=False):
        for sid, (sem, val) in deps.items():
            if skip_self and sem is e.sem:
                continue
            self._wait(e, sem, val)

    def _record(self, reads, writes, tok):
        sid = id(tok[0])
        for b in reads:
            if b.tu.r.get(sid, (None, 0))[1] < tok[1]:
                b.tu.r[sid] = tok
        for b in writes:
            b.tu.w = {sid: tok}
            b.tu.r = {}

    def op(self, eng, fn, reads=(), writes=(), signal=True):
        e = self.eng[eng]
        assert signal or eng == "pe"
        self._emit_waits(e, self._deps(reads, writes, own=e.sem), skip_self=(eng == "pe"))
        if signal:
            e.count += 1
            sem = e.sem
            e.thunks.append(lambda h, fn=fn, sem=sem: fn(h).then_inc(sem, 1))
            tok = (e.sem, e.count)
        else:
            e.thunks.append(lambda h, fn=fn: fn(h))
            tok = (e.sem, e.count + 1)
        self._record(reads, writes, tok)
        self.n_inst += 1

    def dma(self, eng, out, in_, reads=(), writes=()):
        e = self.eng[eng]
        self._emit_waits(e, self._deps(reads, writes))
        slot = e.dma_pool[e.dma_next]
        e.dma_next = (e.dma_next + 1) % len(e.dma_pool)
        sem = slot[0]
        if slot[1] > 0:
            self._wait(e, sem, slot[1] * 16)
        slot[1] += 1
        e.thunks.append(lambda h, out=out, in_=in_, sem=sem: h.dma_start(out=out, in_=in_).then_inc(sem, 16))
        self._record(reads, writes, (sem, slot[1] * 16))
        self.n_inst += 1

    def barrier(self):
        for e in self.eng.values():
            for f in self.eng.values():
                if f is not e and f.count > 0:
                    self._wait(e, f.sem, f.count)
                for sem, cnt in f.dma_pool:
                    if cnt > 0:
                        self._wait(e, sem, cnt * 16)

    def finish(self):
        for sem, val in self.bg_all:
            self._wait(self.eng["sp"], sem, val)
        self.barrier()
        nc = self.nc
        with nc.allow_non_contiguous_dma(reason="small strided state/param transfers"):
            with nc.Block() as block:
                @block.tensor
                def _(h):
                    for th in self.eng["pe"].thunks:
                        th(h)

                @block.scalar
                def _(h):
                    for th in self.eng["act"].thunks:
                        th(h)

                @block.vector
                def _(h):
                    for th in self.eng["dve"].thunks:
                        th(h)

                @block.gpsimd
                def _(h):
                    for th in self.eng["pool"].thunks:
                        th(h)

                @block.sync
                def _(h):
                    for th in self.eng["sp"].thunks:
                        th(h)


class Prog:
    def __init__(self, debug=False, stop=None):
        self.debug = debug
        self.stop = stop
        nc = self.nc = bass.Bass("TRN2", target_bir_lowering=False)
        self.fw = FW(nc)
        self.sb_lo = (nc.sbuf_base + 63) // 64 * 64
        self.sb_hi = nc.sbuf_top
        self.sb_ptr = self.sb_lo
        self.nalloc = 0
        self.decl_io()
        self.psall = nc.alloc_psum_tensor("psall", [128, 8, 512], F32)
        self.psum = [Buf(BankView(self.psall, i, 1), excl=True) for i in range(8)]
        self.psr = Ring(self.psum)
        self.pso = Ring(self.psum[0:2])
        self.pss = Ring(self.psum[2:8])

    def sb(self, shape, dt, n=1):
        bufs = []
        per = int(np.prod(shape[1:])) * (4 if dt == F32 else 2)
        per = (per + 63) // 64 * 64
        for _ in range(n):
            assert self.sb_ptr + per <= self.sb_hi, ("SBUF overflow", self.sb_ptr, per, self.sb_hi)
            t = self.nc.alloc_sbuf_tensor_at(f"sb{self.nalloc}", list(shape), dt, offset=self.sb_ptr)
            self.nalloc += 1
            self.sb_ptr += per
            bufs.append(Buf(t))
        return bufs[0] if n == 1 else Ring(bufs)

    def phase_begin(self):
        self.fw.barrier()
        self.sb_ptr = self.sb_persist
        for b in self.psum:
            b.tu = TU()

    def din(self, name, shape, dt=F32):
        return self.nc.dram_tensor(name, list(shape), dt, kind="ExternalInput").ap()

    def dout(self, name, shape, dt=F32):
        return self.nc.dram_tensor(name, list(shape), dt, kind="ExternalOutput").ap()

    def dscr(self, name, shape, dt):
        if self.debug:
            return self.nc.dram_tensor(name, list(shape), dt, kind="ExternalOutput").ap()
        return self.nc.dram_tensor(name, list(shape), dt).ap()

    def decl_io(self):
        self.xs_in = self.din("x_s", [NS, D])
        self.xp_in = self.din("x_p", [2, NP, D])
        self.cwk = self.din("c_wk", [L, PAST, 128])
        self.cwv = self.din("c_wv", [L, PAST, 128])
        self.cckv = self.din("c_ckv", [L, PAST, 128])
        self.ckr = self.din("c_kr", [L, PAST, 32])
        self.pvec = self.din("pvec", [128, PV_N])
        self.w1 = self.din("w1", [L, 128, 8, C1])
        self.wuq = self.din("wuq", [L, 128, 2, 768])
        self.wukv = self.din("wukv", [L, 128, 512])
        self.wout = self.din("wout", [L, 128, 8, D])
        self.wug = self.din("wug", [L, 11, 128, 2 * 8 * 256])
        self.wdn = self.din("wdn", [L, 128, NFC, D])
        self.wada = self.din("wada", [L, 128, 8, 6 * D])
        self.ropeA = self.din("ropeA", [2, 128, NS])
        self.ropeC = self.din("ropeC", [2, 128, NS])
        self.maskc = self.din("maskc", [128, 6, 512], BF16)
        self.dft64 = self.din("dft64", [128, 256], BF16)
        self.dftp = self.din("dftp", [2, 128, 2, 256], BF16)
        self.dfts = self.din("dfts", [2, 8, 128, 32, 512], BF16)
        self.y_s = self.dout("y_s", [NQH, D])
        self.y_p = self.dout("y_p", [2, NP, D])
        self.o_wk = self.dout("o_wk", [2, L, NP, 128])
        self.o_wv = self.dout("o_wv", [2, L, NP, 128])
        self.o_ckv = self.dout("o_ckv", [2, L, NP, 128])
        self.o_kr = self.dout("o_kr", [2, L, NP, 32])
        self.X0 = self.dscr("X0", [8, 128, NT], F32)
        self.X1 = self.dscr("X1", [8, 128, NT], F32)
        self.QA = self.dscr("QA", [4, 128, NT], BF16)
        self.KA = self.dscr("KA", [2, 128, NK], BF16)
        self.VA = self.dscr("VA", [NK // 128, 128, 512], BF16)
        self.QC = self.dscr("QC", [4, 128, NT], BF16)
        self.KC = self.dscr("KC", [4, 128, NK], BF16)
        self.VC = self.dscr("VC", [NK // 128, 128, 512], BF16)
        self.Z = self.dscr("Z", [NT // 128, 128, 512], BF16)
        self.MIX = self.dscr("MIX", [8, 128, NT], BF16)
        self.w1b = self.dscr("w1b", [L, 128, 8, C1], BF16)
        self.wuqb = self.dscr("wuqb", [L, 128, 2, 768], BF16)
        self.wukvb = self.dscr("wukvb", [L, 128, 512], BF16)
        self.woutb = self.dscr("woutb", [L, 128, 8, D], BF16)
        self.wugb = self.dscr("wugb", [L, 11, 128, 2 * 8 * 256], BF16)
        self.wdnb = self.dscr("wdnb", [L, 128, NFC, D], BF16)

    def mm(self, out, pairs, pbuf, reads, start=True, stop=True, signal=True):
        n = len(pairs)
        for i, (lhsT, rhs) in enumerate(pairs):
            st = start and i == 0
            sp = stop and i == n - 1
            self.fw.op("pe", lambda h, o=out, a=lhsT, b=rhs, st=st, sp=sp: h.matmul(o, lhsT=a, rhs=b, start=st, stop=sp),
                       reads=reads, writes=[pbuf], signal=(signal and i == n - 1))

    def act(self, out, in_, func, reads, writes, bias=0.0, scale=1.0):
        self.fw.op("act", lambda h: h.activation(out=out, in_=in_, func=func, bias=bias, scale=scale),
                   reads=reads, writes=writes)

    def ts(self, eng, out, in0, s1, s2, op0, op1, reads, writes):
        if s2 is None:
            self.fw.op(eng, lambda h: h.tensor_scalar(out=out, in0=in0, scalar1=s1, scalar2=None, op0=op0),
                       reads=reads, writes=writes)
        else:
            self.fw.op(eng, lambda h: h.tensor_scalar(out=out, in0=in0, scalar1=s1, scalar2=s2, op0=op0, op1=op1),
                       reads=reads, writes=writes)

    def stt(self, eng, out, in0, sc, in1, op0, op1, reads, writes):
        self.fw.op(eng, lambda h: h.scalar_tensor_tensor(out=out, in0=in0, scalar=sc, in1=in1, op0=op0, op1=op1),
                   reads=reads, writes=writes)

    def tt(self, eng, out, in0, in1, op, reads, writes):
        self.fw.op(eng, lambda h: h.tensor_tensor(out=out, in0=in0, in1=in1, op=op), reads=reads, writes=writes)

    def cp(self, eng, out, in_, reads, writes):
        if eng == "act":
            self.fw.op("act", lambda h: h.activation(out=out, in_=in_, func=AF.Copy), reads=reads, writes=writes)
        else:
            self.fw.op(eng, lambda h: h.tensor_copy(out, in_), reads=reads, writes=writes)

    def memset(self, eng, ap, val, buf):
        self.fw.op(eng, lambda h: h.memset(ap, val), writes=[buf])

    def rsqrt_ps(self, out, ps_ap, ps_buf, scale, rows=slice(0, 128)):
        self.act(out, ps_ap, AF.Sqrt, [ps_buf], [self._obuf], bias=self.epsb[rows, 0:1], scale=scale)
        self.recip(out, out, [self._obuf], [self._obuf])

    def recip_pool(self, ap, buf, rows, w):
        self.recip(ap, ap, [buf], [buf])

    def recip(self, out, in_, reads, writes):
        self.fw.op("dve", lambda h: h.reciprocal(out=out, in_=in_), reads=reads, writes=writes)

    def ld(self, out, in_, buf, q="sp"):
        self.fw.dma(q, out, in_, writes=[buf])

    def st(self, out, in_, buf, q="sp"):
        self.fw.dma(q, out, in_, reads=[buf])

    def transpose(self, out, in_, pbuf, reads):
        self.fw.op("pe", lambda h: h.transpose(out, in_, self.ident[:]), reads=list(reads) + [self.ident], writes=[pbuf])

    def setup_persistent(self):
        self.ident = self.sb([128, 128], F32)
        self.ones = self.sb([128, 128], BF16)
        self.pv = self.sb([128, PV_N], F32)
        self.mask = self.sb([128, 6, 512], BF16)
        self.d64 = self.sb([128, 256], BF16)
        self.dp = self.sb([128, 2, 2, 256], BF16)
        self.mod = self.sb([128, L, 2, 48], F32)
        self.s1 = self.sb([128, L, 2, 8], F32)
        self.s2 = self.sb([128, L, 2, 8], F32)
        self.esink = self.sb([128, L * 8], F32)
        self.silc = self.sb([128, 8, 2], F32)
        self.epsb = self.sb([128, 1], F32)
        self.identb = self.sb([128, 128], BF16)
        self.neg1 = self.sb([128, 512], F32)
        self.negh = self.sb([128, 512], F32)
        self.sb_persist = self.sb_ptr
        self.memset("pool", self.neg1[:], -1.0, self.neg1)
        self.memset("pool", self.negh[:], -0.5, self.negh)
        self.memset("dve", self.epsb[:], EPS, self.epsb)
        self.memset("pool", self.ident[:], 1.0, self.ident)
        self.fw.op("pool", lambda h: h.affine_select(out=self.ident[:], in_=self.ident[:], pattern=[[-1, 128]],
                                                     compare_op=ALU.is_equal, fill=0.0, base=0, channel_multiplier=1),
                   reads=[self.ident], writes=[self.ident])
        self.memset("dve", self.ones[:], 1.0, self.ones)
        self.cp("dve", self.identb[:], self.ident[:], [self.ident], [self.identb])
        self.ld(self.pv[:], self.pvec, self.pv)
        self.ld(self.mask[:], self.maskc, self.mask)
        self.ld(self.d64[:], self.dft64, self.d64)
        self.ld(self.dp[:], self.dftp.rearrange("c p n f -> p c n f"), self.dp)
        self.act(self.esink[:], self.pv[:, PV_SINK:PV_SINK + L * 8], AF.Exp, [self.pv], [self.esink])
        self.act(self.silc[:], self.pv[:, PV_C:PV_C + 16].rearrange("p (k g) -> p k g", g=2), AF.Silu, [self.pv], [self.silc])

    def pvs(self, off, l, width, idx):
        c = off + l * width + idx
        return self.pv[:, c:c + 1]

    def prologue(self):
        fw = self.fw
        for k in range(0, 8, 2):
            fw.dma("pool", self.w1b[0, :, k:k + 2, :], self.w1[0, :, k:k + 2, :])
        fw.dma("pool", self.wuqb[0], self.wuq[0])
        fw.dma("pool", self.wukvb[0], self.wukv[0])

    def cast_bg(self, stage):
        fw = self.fw
        if stage == 0:
            fw.dma_bg("pool", self.woutb[0], self.wout[0], "out0")
            for g0 in range(0, 11, 3):
                g1 = min(11, g0 + 3)
                fw.dma_bg("pool", self.wugb[0, g0:g1], self.wug[0, g0:g1], "ffn0")
            for c in range(0, NFC, 11):
                fw.dma_bg("pool", self.wdnb[0, :, c:c + 11, :], self.wdn[0, :, c:c + 11, :], "ffn0")
        elif stage == 1:
            fw.wait_bg("pool", "out0")
            fw.wait_bg("pool", "ffn0")
            for k in range(0, 8, 4):
                fw.dma_bg("pool", self.w1b[1, :, k:k + 4, :], self.w1[1, :, k:k + 4, :], "in1")
            fw.dma_bg("pool", self.wuqb[1], self.wuq[1], "in1")
            fw.dma_bg("pool", self.wukvb[1], self.wukv[1], "in1")
            fw.dma_bg("pool", self.woutb[1], self.wout[1], "out1")
        elif stage == 2:
            fw.wait_bg("pool", "in1")
            fw.wait_bg("pool", "out1")
            for g0 in range(0, 11, 3):
                g1 = min(11, g0 + 3)
                fw.dma_bg("pool", self.wugb[1, g0:g1], self.wug[1, g0:g1], "ffn1")
            for c in range(0, NFC, 11):
                fw.dma_bg("pool", self.wdnb[1, :, c:c + 11, :], self.wdn[1, :, c:c + 11, :], "ffn1")

    def prologue2(self):
        fw = self.fw
        wr = self.sb([128, 8, 512], F32, n=2)
        modrow = self.sb([2, 6 * D], F32)
        for l in range(L):
            for blk in range(12):
                wb = wr.next()
                self.ld(wb[:], self.wada[l, :, :, blk * 512:(blk + 1) * 512], wb)
                pr_ = self.psr.next()
                self.mm(pr_[0:2, :], [(self.silc[:, k, :], wb[:, k, :]) for k in range(8)], pr_, [wb, self.silc])
                self.cp("act" if blk % 2 else "dve", modrow[0:2, blk * 512:(blk + 1) * 512], pr_[0:2, :], [pr_], [modrow])
            pm = self.psr.next()
            for q in range(48):
                self.fw.op("pe", lambda h, q=q, pm=pm: h.transpose(pm[:, q * 2:q * 2 + 2], modrow[0:2, q * 128:(q + 1) * 128], self.ident[0:2, 0:2]),
                           reads=[modrow, self.ident], writes=[pm])
            for g in range(2):
                self.tt("dve", self.mod[:, l, g, :], pm[:, 0:96].rearrange("p (q g) -> p q g", g=2)[:, :, g],
                        self.pv[:, PV_BADA + l * 48:PV_BADA + (l + 1) * 48], ALU.add, [pm, self.pv], [self.mod])
                self.stt("dve", self.s1[:, l, g, :], self.mod[:, l, g, 8:16], 1.0,
                         self.pv[:, PV_GMIX + l * 8:PV_GMIX + (l + 1) * 8], ALU.add, ALU.mult, [self.mod, self.pv], [self.s1])
                self.stt("dve", self.s2[:, l, g, :], self.mod[:, l, g, 32:40], 1.0,
                         self.pv[:, PV_GFFN + l * 8:PV_GFFN + (l + 1) * 8], ALU.add, ALU.mult, [self.mod, self.pv], [self.s2])
        if self.debug:
            dm = self.dout("dbg_mod", [128, L * 2 * 48])
            self.st(dm, self.mod[:].rearrange("p l g q -> p (l g q)"), self.mod)
        xin = self.sb([128, 4, D], F32, n=2)
        xo = self.sb([128, 8, 512], F32, n=2)
        for seg in SEGS:
            src = self.xs_in if seg["pb"] is None else self.xp_in[seg["pb"]]
            for t0 in range(0, seg["n"], 512):
                W = min(512, seg["n"] - t0)
                nb = W // 128
                xi = xin.next()
                self.ld(xi[:, 0:nb, :], src[t0:t0 + W, :].rearrange("(b p) f -> p b f", p=128), xi)
                xt = xo.next()
                for k in range(8):
                    ps = self.psr.next()
                    for b in range(nb):
                        self.transpose(ps[:, b * 128:(b + 1) * 128], xi[:, b, k * 128:(k + 1) * 128], ps, [xi])
                    self.cp("dve" if k % 2 else "act", xt[:, k, 0:W], ps[:, 0:W], [ps], [xt])
                self.st(self.X0[:, :, seg["off"] + t0:seg["off"] + t0 + W].rearrange("k p t -> p k t"), xt[:, :, 0:W], xt)

    def norm_mod(self, xt, W, sc, sh, hT, sq, rstd, tmpr, presquared=False):
        if not presquared:
            self.act(sq[:, :, 0:W], xt[:, :, 0:W], AF.Square, [xt], [sq])
        ss = self.psr.next()
        self.mm(ss[:, 0:W], [(self.ones[:], sq[:, k, 0:W]) for k in range(8)], ss, [self.ones, sq])
        self._obuf = rstd
        self.rsqrt_ps(rstd[:, 0:W], ss[:, 0:W], ss, 1.0 / D)
        for k in range(8):
            tmp = tmpr.next()
            self.stt("dve", tmp[:, 0:W], xt[:, k, 0:W], sc(k), rstd[:, 0:W], ALU.mult, ALU.mult, [xt, rstd], [tmp])
            if sh is None:
                self.cp("act", hT[:, k, 0:W], tmp[:, 0:W], [tmp], [hT])
            else:
                self.act(hT[:, k, 0:W], tmp[:, 0:W], AF.Identity, [tmp], [hT], bias=sh(k), scale=1.0)

    def mla_kv(self, ckvn, W, KCt, VCt, wukv):
        nb = W // 128
        for h in range(4):
            ps = self.psr.next()
            self.mm(ps[0:64, 0:W], [(wukv[:, h * 64:(h + 1) * 64], ckvn[:, 0:W])], ps, [wukv, ckvn])
            self.cp("act" if h % 2 else "dve", KCt[0:64, h, 0:W], ps[0:64, 0:W], [ps], [KCt])
        for b2 in range(0, nb, 2):
            ps = self.psr.next()
            for b in range(b2, min(b2 + 2, nb)):
                self.mm(ps[:, (b - b2) * 256:(b - b2 + 1) * 256], [(ckvn[:, b * 128:(b + 1) * 128], wukv[:, 256:512])],
                        ps, [wukv, ckvn])
            n2 = min(2, nb - b2)
            for h in range(4):
                off = 0 if h % 2 == 0 else 64
                self.cp("dve" if h % 2 else "act", VCt[:, b2:b2 + n2, h, off:off + 64],
                        ps[:, 0:n2 * 256].rearrange("p (b f) -> p b f", f=256)[:, :, h * 64:(h + 1) * 64], [ps], [VCt])

    def phase_a(self, l):
        self.phase_begin()
        self.fw.wait_bg("sp", f"in{l}")
        self.psr = Ring(self.psum[3:8])
        try:
            self._phase_a(l)
        finally:
            self.psr = Ring(self.psum)

    def _phase_a(self, l):
        w1 = self.sb([128, 8, C1], BF16)
        wuq = self.sb([128, 2, 768], BF16)
        wukv = self.sb([128, 512], BF16)
        for k in range(8):
            self.ld(w1[:, k, :], self.w1b[l, :, k, :], w1)
        self.ld(wuq[:], self.wuqb[l], wuq)
        self.ld(wukv[:], self.wukvb[l], wukv)
        xr = self.sb([128, 8, 512], F32, n=2)
        sq = self.sb([128, 8, 512], BF16)
        hr = self.sb([128, 8, 512], BF16, n=2)
        tmpr = self.sb([128, 512], F32, n=3)
        rstd = self.sb([128, 512], F32)
        rstq = self.sb([128, 512], F32)
        rstk = self.sb([128, 512], F32)
        rA = self.sb([128, 2, 512], F32)
        rC = self.sb([128, 2, 512], F32)
        QAt = self.sb([128, 4, 512], BF16)
        KAt = self.sb([128, 2, 512], BF16)
        VAt = self.sb([128, 4, 4, 128], BF16)
        QCt = self.sb([128, 4, 512], BF16)
        KCt = self.sb([128, 4, 512], BF16)
        VCt = self.sb([128, 4, 4, 128], BF16)
        Zt = self.sb([128, 4, 512], BF16)
        fbT = self.sb([128, 2, 512], BF16)
        cqn = self.sb([128, 2, 512], BF16)
        sqq = self.sb([128, 2, 512], BF16)
        ckvn = self.sb([128, 512], BF16)
        sqk = self.sb([128, 512], BF16)
        t1r = self.sb([128, 512], F32, n=2)
        t2r = self.sb([128, 512], F32, n=2)
        stF = self.sb([128, 512], F32, n=2)
        stT = self.sb([128, 4, 128], F32, n=2)
        krF = self.sb([128, 512], F32)
        vaF = self.sb([128, 4, 128], F32)
        ctxin = self.sb([128, 2, 128], F32, n=2)
        self.memset("pool", VAt[:], 1.0, VAt)
        self.memset("pool", VCt[:], 1.0, VCt)
        self.memset("dve", krF[:], 0.0, krF)
        self.memset("dve", KCt[:], 0.0, KCt)
        self.memset("dve", QCt[:], 0.0, QCt)

        def proj(col, M, hT, W):
            ps = self.psr.next()
            self.mm(ps[0:M, 0:W], [(w1[:, k, col:col + M], hT[:, k, 0:W]) for k in range(8)], ps, [w1, hT])
            return ps

        def rope(ps_q, ps_p, tab, rows, W, dst_ap, dst_buf, extra_dsts=()):
            t1 = t1r.next()
            t2 = t2r.next()
            self.tt("dve", t1[rows, 0:W], ps_q[rows, 0:W], tab[rows, 0, 0:W], ALU.mult, [ps_q, tab], [t1])
            self.tt("dve", t2[rows, 0:W], ps_p[rows, 0:W], tab[rows, 1, 0:W], ALU.mult, [ps_p, tab], [t2])
            self.tt("pool", dst_ap, t1[rows, 0:W], t2[rows, 0:W], ALU.add, [t1, t2], [dst_buf])
            for d in extra_dsts:
                self.tt("pool", d, t1[rows, 0:W], t2[rows, 0:W], ALU.add, [t1, t2], [dst_buf])

        def state_out(srcF, W, dst_rows_fn, c0, c1):
            nb = W // 128
            ps = self.psr.next()
            for b in range(nb):
                self.transpose(ps[:, b * 128:(b + 1) * 128], srcF[:, b * 128:(b + 1) * 128], ps, [srcF])
            tt_ = stT.next()
            self.cp("dve", tt_[:, 0:nb, :], ps[:, 0:W].rearrange("p (b f) -> p b f", f=128), [ps], [tt_])
            self.st(dst_rows_fn, tt_[:, 0:nb, c0:c1], tt_)

        tl = [(seg, t0, min(512, seg["n"] - t0)) for seg in SEGS for t0 in range(0, seg["n"], 512)]

        def pre_x(i):
            seg_, t0_, W_ = tl[i]
            xt_ = xr.next()
            c_ = seg_["off"] + t0_
            self.ld(xt_[:, :, 0:W_], self.X0[:, :, c_:c_ + W_].rearrange("k p t -> p k t"), xt_)
            return xt_

        def pre_norm(i, xt_):
            seg_, t0_, W_ = tl[i]
            g_ = seg_["grp"]
            h_ = hr.next()
            self.norm_mod(xt_, W_, lambda k: self.s1[:, l, g_, k:k + 1], lambda k: self.mod[:, l, g_, k:k + 1],
                          h_, sq, rstd, tmpr)
            return h_

        hT_next = pre_norm(0, pre_x(0))
        xt_next = pre_x(1)
        for ti, (seg, t0, W) in enumerate(tl):
            if True:
                g = seg["grp"]
                nb = W // 128
                col = seg["off"] + t0
                kcol = seg["koff"] + t0
                hT = hT_next
                if seg["rope"]:
                    self.ld(rA[:, :, 0:W], self.ropeA[:, :, t0:t0 + W].rearrange("c p t -> p c t"), rA)
                    self.ld(rC[:, :, 0:W], self.ropeC[:, :, t0:t0 + W].rearrange("c p t -> p c t"), rC)
                need_q = not (seg["pb"] is None and l == L - 1 and t0 >= NQL)
                def proj_h(bank, col_, M_):
                    ps_ = self.psum[bank]
                    self.mm(ps_[0:M_, 0:W], [(w1[:, k, col_:col_ + M_], hT[:, k, 0:W]) for k in range(8)], ps_, [w1, hT])
                    return ps_
                if need_q:
                    pq0 = proj_h(0, C_CQ, 128)
                    pq1 = proj_h(1, C_CQ + 128, 64)
                    self.act(sqq[:, 0, 0:W], pq0[:, 0:W], AF.Square, [pq0], [sqq])
                    self.act(sqq[0:64, 1, 0:W], pq1[0:64, 0:W], AF.Square, [pq1], [sqq])
                pk = proj_h(2, C_CKV, 128)
                self.act(sqk[:, 0:W], pk[:, 0:W], AF.Square, [pk], [sqk])
                if need_q:
                    for c in range(4):
                        pq = proj(C_QA + c * 128, 128, hT, W)
                        if seg["rope"]:
                            pp = proj(C_QAP + c * 128, 128, hT, W)
                            rope(pq, pp, rA, slice(0, 128), W, QAt[:, c, 0:W], QAt)
                        else:
                            self.cp("act", QAt[:, c, 0:W], pq[:, 0:W], [pq], [QAt])
                    self.st(self.QA[:, :, col:col + W].rearrange("c p t -> p c t"), QAt[:, :, 0:W], QAt)
                if need_q:
                    ss = self.psr.next()
                    self.mm(ss[:, 0:W], [(self.ones[:], sqq[:, 0, 0:W]), (self.ones[0:64, :], sqq[0:64, 1, 0:W])], ss, [self.ones, sqq])
                    self._obuf = rstq
                    self.rsqrt_ps(rstq[:, 0:W], ss[:, 0:W], ss, 1.0 / 192)
                ss = self.psr.next()
                self.mm(ss[:, 0:W], [(self.ones[:], sqk[:, 0:W])], ss, [self.ones, sqk])
                self._obuf = rstk
                self.rsqrt_ps(rstk[:, 0:W], ss[:, 0:W], ss, 1.0 / 128)
                if need_q:
                    self.stt("dve", cqn[:, 0, 0:W], pq0[:, 0:W], self.pvs(PV_GCQ, l, 2, 0), rstq[:, 0:W], ALU.mult, ALU.mult,
                             [pq0, rstq, self.pv], [cqn])
                    self.stt("dve", cqn[0:64, 1, 0:W], pq1[0:64, 0:W], self.pv[0:64, PV_GCQ + l * 2 + 1:PV_GCQ + l * 2 + 2],
                             rstq[0:64, 0:W], ALU.mult, ALU.mult, [pq1, rstq, self.pv], [cqn])
                if seg["pb"] is not None:
                    sf = stF.next()
                    self.stt("dve", sf[:, 0:W], pk[:, 0:W], self.pvs(PV_GCKV, l, 1, 0), rstk[:, 0:W], ALU.mult, ALU.mult,
                             [pk, rstk, self.pv], [sf])
                    self.cp("act", ckvn[:, 0:W], sf[:, 0:W], [sf], [ckvn])
                    state_out(sf, W, self.o_ckv[seg["pb"], l, t0:t0 + W, :].rearrange("(b p) f -> p b f", p=128), 0, 128)
                else:
                    self.stt("dve", ckvn[:, 0:W], pk[:, 0:W], self.pvs(PV_GCKV, l, 1, 0), rstk[:, 0:W], ALU.mult, ALU.mult,
                             [pk, rstk, self.pv], [ckvn])
                for kv in range(2):
                    pq = proj(C_KD0 + kv * 128, 128, hT, W)
                    if seg["rope"]:
                        pp = proj(C_KD0P + kv * 128, 128, hT, W)
                        rope(pq, pp, rA, slice(0, 128), W, KAt[:, kv, 0:W], KAt)
                    else:
                        self.cp("act", KAt[:, kv, 0:W], pq[:, 0:W], [pq], [KAt])
                self.st(self.KA[:, :, kcol:kcol + W].rearrange("c p t -> p c t"), KAt[:, :, 0:W], KAt)
                if seg["pb"] is not None:
                    pq = proj(C_KAN, 128, hT, W)
                    sf = stF.next()
                    self.cp("act", sf[:, 0:W], pq[:, 0:W], [pq], [sf])
                    state_out(sf, W, self.o_wk[seg["pb"], l, t0:t0 + W, :].rearrange("(b p) f -> p b f", p=128), 0, 128)
                if ti + 1 < len(tl):
                    hT_next = pre_norm(ti + 1, xt_next)
                    if ti + 2 < len(tl):
                        xt_next = pre_x(ti + 2)
                ps = self.psr.next()
                for b in range(nb):
                    self.mm(ps[:, b * 128:(b + 1) * 128],
                            [(hT[:, k, b * 128:(b + 1) * 128], w1[:, k, C_VA:C_VA + 128]) for k in range(8)], ps, [w1, hT])
                psv = ps[:, 0:W].rearrange("p (b f) -> p b f", f=128)
                for s_ in range(4):
                    kv, off = s_ // 2, (0 if s_ % 2 == 0 else 64)
                    self.cp("act" if s_ % 2 else "dve", VAt[:, 0:nb, s_, off:off + 64], psv[:, :, kv * 64:(kv + 1) * 64], [ps], [VAt])
                if seg["pb"] is not None:
                    self.cp("dve", vaF[:, 0:nb, :], psv, [ps], [vaF])
                    self.st(self.o_wv[seg["pb"], l, t0:t0 + W, :].rearrange("(b p) f -> p b f", p=128), vaF[:, 0:nb, :], vaF)
                self.st(self.VA[kcol // 128:kcol // 128 + nb].rearrange("c p f -> p c f"),
                        VAt[:, 0:nb].rearrange("p b s f -> p b (s f)"), VAt)
                for f in range(2):
                    pq = proj(C_FB + f * 128, 128, hT, W)
                    self.cp("act", fbT[:, f, 0:W], pq[:, 0:W], [pq], [fbT])
                for b in range(nb):
                    ps = self.psr.next()
                    for f in range(2):
                        self.mm(ps[:, f * 256:(f + 1) * 256], [(fbT[:, f, b * 128:(b + 1) * 128], self.d64[:])], ps, [fbT, self.d64])
                    self.cp("dve" if b % 2 else "act", Zt[:, b, :], ps[:, :], [ps], [Zt])
                self.st(self.Z[col // 128:col // 128 + nb].rearrange("c p f -> p c f"), Zt[:, 0:nb, :], Zt)
                if need_q:
                    for h in range(4):
                        def uq(c0):
                            ps_ = self.psr.next()
                            self.mm(ps_[0:96, 0:W], [(wuq[:, 0, c0:c0 + 96], cqn[:, 0, 0:W]),
                                                     (wuq[0:64, 1, c0:c0 + 96], cqn[0:64, 1, 0:W])], ps_, [wuq, cqn])
                            return ps_
                        pq = uq(h * 96)
                        if seg["rope"]:
                            pp = uq(384 + h * 96)
                            rope(pq, pp, rC, slice(0, 96), W, QCt[0:96, h, 0:W], QCt)
                        else:
                            self.cp("act", QCt[0:96, h, 0:W], pq[0:96, 0:W], [pq], [QCt])
                    self.st(self.QC[:, :, col:col + W].rearrange("c p t -> p c t"), QCt[:, :, 0:W], QCt)
                self.mla_kv(ckvn, W, KCt, VCt, wukv)
                pq = proj(C_KR, 96, hT, W)
                rows = slice(64, 96)
                if seg["rope"]:
                    pp = proj(C_KRP, 96, hT, W)
                    rope(pq, pp, rC, rows, W, KCt[rows, 0, 0:W], KCt, [KCt[rows, h, 0:W] for h in range(1, 4)])
                else:
                    for h in range(4):
                        self.cp("act" if h % 2 else "dve", KCt[rows, h, 0:W], pq[rows, 0:W], [pq], [KCt])
                    self.cp("act", krF[rows, 0:W], pq[rows, 0:W], [pq], [krF])
                    state_out(krF, W, self.o_kr[seg["pb"], l, t0:t0 + W, :].rearrange("(b p) f -> p b f", p=128), 64, 96)
                self.st(self.KC[:, :, kcol:kcol + W].rearrange("c p t -> p c t"), KCt[:, :, 0:W], KCt)
                self.st(self.VC[kcol // 128:kcol // 128 + nb].rearrange("c p f -> p c f"),
                        VCt[:, 0:nb].rearrange("p b s f -> p b (s f)"), VCt)
        kc0 = NS
        W = PAST
        nb = 2
        ci = ctxin.next()
        self.ld(ci[:], self.cwk[l].rearrange("(b p) f -> p b f", p=128), ci)
        ps = self.psr.next()
        for b in range(nb):
            self.transpose(ps[:, b * 128:(b + 1) * 128], ci[:, b, :], ps, [ci])
        for kv in range(2):
            src = slice(kv * 64, kv * 64 + 64)
            self.cp("dve", KAt[0:64, kv, 0:W], ps[src, 0:W], [ps], [KAt])
            self.cp("act", KAt[64:128, kv, 0:W], ps[src, 0:W], [ps], [KAt])
        self.st(self.KA[:, :, kc0:kc0 + W].rearrange("c p t -> p c t"), KAt[:, :, 0:W], KAt)
        ci = ctxin.next()
        self.ld(ci[:], self.cwv[l].rearrange("(b p) f -> p b f", p=128), ci)
        for s_ in range(4):
            kv, off = s_ // 2, (0 if s_ % 2 == 0 else 64)
            self.cp("act" if s_ % 2 else "dve", VAt[:, 0:nb, s_, off:off + 64], ci[:, :, kv * 64:(kv + 1) * 64], [ci], [VAt])
        self.st(self.VA[kc0 // 128:kc0 // 128 + nb].rearrange("c p f -> p c f"), VAt[:, 0:nb].rearrange("p b s f -> p b (s f)"), VAt)
        ci = ctxin.next()
        self.ld(ci[:], self.cckv[l].rearrange("(b p) f -> p b f", p=128), ci)
        ps = self.psr.next()
        for b in range(nb):
            self.transpose(ps[:, b * 128:(b + 1) * 128], ci[:, b, :], ps, [ci])
        self.cp("act", ckvn[:, 0:W], ps[:, 0:W], [ps], [ckvn])
        self.mla_kv(ckvn, W, KCt, VCt, wukv)
        ci = ctxin.next()
        self.memset("pool", ci[:], 0.0, ci)
        self.ld(ci[:, :, 64:96], self.ckr[l].rearrange("(b p) f -> p b f", p=128), ci)
        ps = self.psr.next()
        for b in range(nb):
            self.transpose(ps[:, b * 128:(b + 1) * 128], ci[:, b, :], ps, [ci])
        for h in range(4):
            self.cp("act" if h % 2 else "dve", KCt[64:96, h, 0:W], ps[64:96, 0:W], [ps], [KCt])
        self.st(self.KC[:, :, kc0:kc0 + W].rearrange("c p t -> p c t"), KCt[:, :, 0:W], KCt)
        self.st(self.VC[kc0 // 128:kc0 // 128 + nb].rearrange("c p f -> p c f"), VCt[:, 0:nb].rearrange("p b s f -> p b (s f)"), VCt)

    def attn_tile(self, heads, W, scale, pr, recr, reads, LA=3):
        items = []
        for hi, hd in enumerate(heads):
            n = len(hd["keys"])
            for i, key in enumerate(hd["keys"]):
                items.append((hi, i, n, key))
        sb = {}

        def issue_s(j):
            hi, i, n, (k_ap, v_ap, m_ap) = items[j]
            S = self.pss.next()
            self.mm(S[:, 0:W], [(k_ap, heads[hi]["q"])], S, reads)
            sb[j] = S

        for j in range(min(LA, len(items))):
            issue_s(j)
        O = None
        for j, (hi, i, n, (k_ap, v_ap, m_ap)) in enumerate(items):
            hd = heads[hi]
            if i == 0:
                O = self.pso.next()
            S = sb.pop(j)
            P = pr.next()
            self.act(P[:, 0:W], S[:, 0:W], AF.Exp, [S], [P], scale=scale)
            if m_ap is not None:
                self.tt("pool", P[:, 0:W], P[:, 0:W], m_ap, ALU.mult, [P, self.mask], [P])
            if j + LA < len(items):
                issue_s(j + LA)
            self.mm(O[:, 0:W], [(v_ap, P[:, 0:W])], O, reads + [P], start=(i == 0), stop=(i == n - 1))
            if i == n - 1:
                orows, drows = (slice(0, 64), slice(64, 128)) if hd["vfirst"] else (slice(64, 128), slice(0, 64))
                rec = recr.next()
                if hd["sink"] is not None:
                    self.ts("dve", rec[orows, 0:W], O[drows, 0:W], hd["sink"](drows), None, ALU.add, None, [O, self.esink], [rec])
                    self.recip(rec[orows, 0:W], rec[orows, 0:W], [rec], [rec])
                else:
                    self.recip(rec[orows, 0:W], O[drows, 0:W], [O], [rec])
                self.tt("dve", hd["dst"](orows), O[orows, 0:W], rec[orows, 0:W], ALU.mult, [O, rec], [hd["dst_buf"]])

    def attn_tile_grp(self, heads, W, scale, pgr, recr, reads):
        items = []
        for hi, hd in enumerate(heads):
            ng = len(hd["groups"])
            for gi, grp in enumerate(hd["groups"]):
                items.append((hi, gi, ng, grp))
        slots = [2, 4, 6]
        LA = len(slots) - 1
        st = {"n": 0}
        sb = {}

        def issue_s(j):
            hi, gi, ng, grp = items[j]
            assert len(grp["keys"]) <= 2
            b = slots[st["n"] % len(slots)]
            st["n"] += 1
            for off, (k_ap, v_ap) in enumerate(grp["keys"]):
                S = self.psum[b + off]
                pairs = [(k_ap, heads[hi]["q"])]
                if grp["mask"] is not None:
                    pairs.append((self.identb[:], grp["mask"][off]))
                self.mm(S[:, 0:W], pairs, S, reads + [self.identb, self.mask])
            sb[j] = b

        for j in range(min(LA, len(items))):
            issue_s(j)
        O = None
        for j, (hi, gi, ng, grp) in enumerate(items):
            hd = heads[hi]
            if gi == 0:
                O = self.pso.next()
            b = sb.pop(j)
            n = len(grp["keys"])
            P = pgr.next()
            self.act(P[:, 0:n, 0:W], self.psall[:, b:b + n, 0:W], AF.Exp, [self.psum[b + i] for i in range(n)], [P], scale=scale)
            if j + LA < len(items):
                issue_s(j + LA)
            self.mm(O[:, 0:W], [(v_ap, P[:, i, 0:W]) for i, (k_ap, v_ap) in enumerate(grp["keys"])], O, reads + [P],
                    start=(gi == 0), stop=(gi == ng - 1))
            if gi == ng - 1:
                orows, drows = (slice(0, 64), slice(64, 128)) if hd["vfirst"] else (slice(64, 128), slice(0, 64))
                rec = recr.next()
                if hd["sink"] is not None:
                    self.ts("dve", rec[orows, 0:W], O[drows, 0:W], hd["sink"](drows), None, ALU.add, None, [O, self.esink], [rec])
                else:
                    self.cp("dve", rec[orows, 0:W], O[drows, 0:W], [O], [rec])
                self.recip_pool(rec[orows, 0:W], rec, orows, W)
                self.tt("dve", hd["dst"](orows), O[orows, 0:W], rec[orows, 0:W], ALU.mult, [O, rec], [hd["dst_buf"]])

    def attn_tile_pair(self, heads, W, scale, ppr, recr, reads, LA=2):
        items = []
        for hi, hd in enumerate(heads):
            ks = hd["keys"]
            assert len(ks) % 2 == 0
            for i in range(0, len(ks), 2):
                items.append((hi, i // 2, len(ks) // 2, ks[i], ks[i + 1]))
        pairs = [2, 4, 6]
        st = {"n": 0}
        sb = {}

        def issue_s(j):
            hi, i, n, ka, kb = items[j]
            b = pairs[st["n"] % 3]
            st["n"] += 1
            for off, key in ((0, ka), (1, kb)):
                S = self.psum[b + off]
                self.mm(S[:, 0:W], [(key[0], heads[hi]["q"])], S, reads)
            sb[j] = b

        for j in range(min(LA, len(items))):
            issue_s(j)
        O = None
        for j, (hi, i, n, ka, kb) in enumerate(items):
            hd = heads[hi]
            if i == 0:
                O = self.pso.next()
            b = sb.pop(j)
            P = ppr.next()
            sv = BankView(self.psall, b, 2)
            self.act(P[:, :, 0:W], sv[:, 0:W], AF.Exp, [self.psum[b], self.psum[b + 1]], [P], scale=scale)
            if j + LA < len(items):
                issue_s(j + LA)
            self.mm(O[:, 0:W], [(ka[1], P[:, 0, 0:W]), (kb[1], P[:, 1, 0:W])], O, reads + [P], start=(i == 0), stop=(i == n - 1))
            if i == n - 1:
                orows, drows = (slice(0, 64), slice(64, 128)) if hd["vfirst"] else (slice(64, 128), slice(0, 64))
                rec = recr.next()
                self.recip(rec[orows, 0:W], O[drows, 0:W], [O], [rec])
                self.tt("dve", hd["dst"](orows), O[orows, 0:W], rec[orows, 0:W], ALU.mult, [O, rec], [hd["dst_buf"]])

    def phase_ba(self, l):
        self.phase_begin()
        KAz = [self.sb([128, 2, NS + PAST], BF16), self.sb([128, 2, NS + PAST], BF16)]
        self.memset("pool", KAz[0][64:128, :, :], 0.0, KAz[0])
        self.memset("dve", KAz[1][0:64, :, :], 0.0, KAz[1])
        VAs = self.sb([128, (NS + PAST) // 128, 4, 128], BF16)
        qr = self.sb([128, 4, 512], BF16, n=2)
        mr = self.sb([128, 4, 512], BF16, n=2)
        pgr = self.sb([128, 2, 512], BF16, n=6)
        recr = self.sb([128, 512], F32, n=4)
        for seg in SEGS:
            nk = seg["nk"]
            nch = nk // 128
            self.ld(KAz[0][0:64, :, 0:nk], self.KA[:, 0:64, seg["koff"]:seg["koff"] + nk].rearrange("c p t -> p c t"), KAz[0])
            self.ld(KAz[1][64:128, :, 0:nk], self.KA[:, 64:128, seg["koff"]:seg["koff"] + nk].rearrange("c p t -> p c t"), KAz[1])
            self.ld(VAs[:, 0:nch].rearrange("p c s f -> p c (s f)"),
                    self.VA[seg["koff"] // 128:seg["koff"] // 128 + nch].rearrange("c p f -> p c f"), VAs)
            qts = qtiles(seg, l)

            def ldq(i_):
                t0_, W_ = qts[i_]
                c_ = seg["off"] + t0_
                q_ = qr.next()
                self.ld(q_[:, :, 0:W_], self.QA[:, :, c_:c_ + W_].rearrange("c p t -> p c t"), q_)
                return q_

            qt_next = ldq(0)
            for ti_, (t0, W) in enumerate(qts):
                col = seg["off"] + t0
                qt = qt_next
                if ti_ + 1 < len(qts):
                    qt_next = ldq(ti_ + 1)
                mt = mr.next()
                n0 = t0 // 128
                heads = []
                for c in range(4):
                    kv = c // 2
                    for half in range(2):
                        h = 2 * c + half
                        rows = slice(half * 64, half * 64 + 64)
                        def kv_(jb):
                            return (KAz[half][:, kv, jb * 128:(jb + 1) * 128], VAs[:, jb, kv * 2 + half, :])
                        groups = []
                        if seg["rope"]:
                            band = [jb for jb in range(n0 - 1, n0 + 5) if 0 <= jb < NS // 128]
                            for i0 in range(0, len(band), 2):
                                sub = band[i0:i0 + 2]
                                r0 = sub[0] - n0 + 1
                                groups.append(dict(keys=[kv_(jb) for jb in sub],
                                                   mask=[self.mask[:, r0 + i_, 0:W] for i_ in range(len(sub))]))
                            groups.append(dict(keys=[kv_(jb) for jb in range(NS // 128, NS // 128 + 2)], mask=None))
                        else:
                            groups.append(dict(keys=[kv_(jb) for jb in range(nch)], mask=None))
                        heads.append(dict(q=qt[:, c, 0:W], groups=groups, vfirst=(half == 0),
                                          sink=(lambda dr, h=h: self.esink[dr, l * 8 + h:l * 8 + h + 1]),
                                          dst=(lambda orr, c=c: mt[orr, c, 0:W]), dst_buf=mt))
                self.attn_tile_grp(heads, W, 0.125, pgr, recr, [KAz[0], KAz[1], VAs, qt])
                self.st(self.MIX[0:4, :, col:col + W].rearrange("c p t -> p c t"), mt[:, :, 0:W], mt)
                if l == 0 and seg["pb"] is None and t0 == 0:
                    pe_ = self.fw.eng["pe"]
                    self.fw._wait(self.fw.eng["pool"], pe_.sem, pe_.count)
                    self.cast_bg(0)

    def phase_bc(self, l):
        self.phase_begin()
        KCs = self.sb([128, 4, NS + PAST], BF16)
        VCs = self.sb([128, (NS + PAST) // 128, 4, 128], BF16)
        qr = self.sb([128, 4, 512], BF16, n=2)
        mr = self.sb([128, 2, 512], BF16, n=2)
        pgr = self.sb([128, 2, 512], BF16, n=6)
        recr = self.sb([128, 512], F32, n=4)
        sc = 96.0 ** -0.5
        for seg in SEGS:
            nk = seg["nk"]
            nch = nk // 128
            self.ld(KCs[:, :, 0:nk], self.KC[:, :, seg["koff"]:seg["koff"] + nk].rearrange("c p t -> p c t"), KCs)
            self.ld(VCs[:, 0:nch].rearrange("p c s f -> p c (s f)"),
                    self.VC[seg["koff"] // 128:seg["koff"] // 128 + nch].rearrange("c p f -> p c f"), VCs)
            qts = qtiles(seg, l)

            def ldq(i_):
                t0_, W_ = qts[i_]
                c_ = seg["off"] + t0_
                q_ = qr.next()
                self.ld(q_[:, :, 0:W_], self.QC[:, :, c_:c_ + W_].rearrange("c p t -> p c t"), q_)
                return q_

            qt_next = ldq(0)
            for ti_, (t0, W) in enumerate(qts):
                col = seg["off"] + t0
                qt = qt_next
                if ti_ + 1 < len(qts):
                    qt_next = ldq(ti_ + 1)
                mt = mr.next()
                heads = []
                for h in range(4):
                    keys = [(KCs[0:96, h, jb * 128:(jb + 1) * 128], VCs[:, jb, h, :]) for jb in range(nch)]
                    groups = [dict(keys=keys[i0:i0 + 2], mask=None) for i0 in range(0, nch, 2)]
                    heads.append(dict(q=qt[0:96, h, 0:W], groups=groups, vfirst=(h % 2 == 0), sink=None,
                                      dst=(lambda orr, h=h: mt[orr, h // 2, 0:W]), dst_buf=mt))
                self.attn_tile_grp(heads, W, sc, pgr, recr, [KCs, VCs, qt])
                self.st(self.MIX[6:8, :, col:col + W].rearrange("c p t -> p c t"), mt[:, :, 0:W], mt)
                if l == 0 and seg["pb"] is None and t0 == 0:
                    pe_ = self.fw.eng["pe"]
                    self.fw._wait(self.fw.eng["pool"], pe_.sem, pe_.count)
                    self.cast_bg(1)
                    self.cast_bg(2)

    def phase_bf(self, l):
        self.phase_begin()
        Zs = self.sb([128, NS // 128, 512], BF16)
        cr = self.sb([128, 8, 512], BF16, n=3)
        sr = self.sb([128, 8, 512], BF16, n=3)
        mr = self.sb([128, 2, 512], BF16, n=2)
        for seg in SEGS:
            n = seg["n"]
            nch = n // 128
            self.ld(Zs[:, 0:nch, :], self.Z[seg["off"] // 128:seg["off"] // 128 + nch].rearrange("c p f -> p c f"), Zs)
            scale = (n * 64.0) ** -0.5
            for t0, W in qtiles(seg, l):
                col = seg["off"] + t0
                mt = mr.next()
                acc = [self.psr.next(), self.psr.next()]
                if seg["rope"]:
                    for gp in range(4):
                        cb, sbb = cr.next(), sr.next()
                        self.ld(cb[:], self.dfts[0, t0 // 512, :, gp * 8:(gp + 1) * 8, :], cb)
                        self.ld(sbb[:], self.dfts[1, t0 // 512, :, gp * 8:(gp + 1) * 8, :], sbb)
                        for f in range(2):
                            pairs = []
                            for j in range(8):
                                nc_ = gp * 8 + j
                                pairs.append((Zs[:, nc_, f * 256:f * 256 + 128], cb[:, j, 0:W]))
                                pairs.append((Zs[:, nc_, f * 256 + 128:f * 256 + 256], sbb[:, j, 0:W]))
                            self.mm(acc[f][:, 0:W], pairs, acc[f], [Zs, cb, sbb], start=(gp == 0), stop=(gp == 3))
                else:
                    for f in range(2):
                        pairs = []
                        for j in range(nch):
                            pairs.append((Zs[:, j, f * 256:f * 256 + 128], self.dp[:, 0, j, 0:W]))
                            pairs.append((Zs[:, j, f * 256 + 128:f * 256 + 256], self.dp[:, 1, j, 0:W]))
                        self.mm(acc[f][:, 0:W], pairs, acc[f], [Zs, self.dp])
                for f in range(2):
                    self.act(mt[:, f, 0:W], acc[f][:, 0:W], AF.Identity, [acc[f]], [mt], scale=scale)
                self.st(self.MIX[4:6, :, col:col + W].rearrange("c p t -> p c t"), mt[:, :, 0:W], mt)

    def phase_bo(self, l):
        self.phase_begin()
        self.fw.wait_bg("sp", f"out{l}")
        wo = self.sb([128, 8, D], BF16)
        for k in range(8):
            self.ld(wo[:, k, :], self.woutb[l, :, k, :], wo)
        mr = self.sb([128, 8, 512], BF16, n=2)
        xr = self.sb([128, 8, 512], F32, n=2)
        xo = self.sb([128, 8, 512], F32, n=2)
        for seg in SEGS:
            g = seg["grp"]
            qts = qtiles(seg, l)

            def ldmx(i_):
                t0_, W_ = qts[i_]
                c_ = seg["off"] + t0_
                m_ = mr.next()
                self.ld(m_[:, :, 0:W_], self.MIX[:, :, c_:c_ + W_].rearrange("c p t -> p c t"), m_)
                x_ = xr.next()
                self.ld(x_[:, :, 0:W_], self.X0[:, :, c_:c_ + W_].rearrange("k p t -> p k t"), x_)
                return m_, x_

            nxt = ldmx(0)
            for ti_, (t0, W) in enumerate(qts):
                col = seg["off"] + t0
                mt, xt = nxt
                if ti_ + 1 < len(qts):
                    nxt = ldmx(ti_ + 1)
                xn = xo.next()
                for m in range(8):
                    ps = self.psr.next()
                    self.mm(ps[:, 0:W], [(wo[:, k, m * 128:(m + 1) * 128], mt[:, k, 0:W]) for k in range(8)], ps, [wo, mt])
                    self.stt("dve", xn[:, m, 0:W], ps[:, 0:W], self.mod[:, l, g, 16 + m:17 + m], xt[:, m, 0:W],
                             ALU.mult, ALU.add, [ps, xt, self.mod], [xn])
                self.st(self.X1[:, :, col:col + W].rearrange("k p t -> p k t"), xn[:, :, 0:W], xn)

    def phase_c(self, l):
        self.phase_begin()
        self.fw.wait_bg("sp", f"ffn{l}")
        wd = self.sb([128, NFC, D], BF16)
        actb = [self.sb([128, 512], BF16) for _ in range(NFC)]
        xr = self.sb([128, 8, 512], F32)
        xsb = self.sb([128, 8, 512], F32)
        sq = self.sb([128, 8, 512], BF16)
        h2r = self.sb([128, 8, 512], BF16, n=2)
        tmpr = self.sb([128, 512], F32, n=2)
        rstd = self.sb([128, 512], F32)
        wgrp = self.sb([128, 2, 8, 256], BF16, n=4)
        cvr = self.sb([128, 512], F32, n=8)
        sgr = self.sb([128, 512], F32, n=3)
        xor_ = self.sb([128, 512], F32, n=2)
        halos = [self.sb([128, 2, NFC, 2], F32), self.sb([128, 2, NFC, 2], F32)]
        hterm = self.sb([128, 2, NFC, 2], F32)
        ht2 = self.sb([128, 2, NFC], F32)
        fin = self.sb([128, 2, NFC], F32)
        fin2 = self.sb([128, 2, NFC], F32)
        actf = self.sb([128, NFC, 2], BF16)
        xf = self.sb([128, 8, 2], F32)
        xfo = self.sb([128, 8, 2], F32)
        NG = NFC // 2
        PF = 3
        cwls = [self.pv[:, base + l * 132:base + (l + 1) * 132].rearrange("p (t a c) -> p t a c", t=3, a=2)
                for base in (PV_CW, PV_CWS)]
        cbl = self.pv[:, PV_CB + l * 44:PV_CB + (l + 1) * 44].rearrange("p (a c) -> p a c", a=2)
        cwbase = [PV_CW]

        def cwv(tap, ci):
            c_ = cwbase[0] + l * 132 + tap * 44 + ci
            return self.pv[:, c_:c_ + 1]

        def cbv(ci):
            c_ = PV_CB + l * 44 + ci
            return self.pv[:, c_:c_ + 1]

        tiles = []
        for seg in SEGS:
            qt_ = qtiles(seg, l)
            full = (qt_[-1][0] + qt_[-1][1] == seg["n"])
            for j, (t0_, W_) in enumerate(qt_):
                tiles.append((seg, j, len(qt_) if full else -1, W_, t0_))
        wq = []

        def tinfo(ti):
            seg, j, ntile, W, t0_ = tiles[ti]
            return seg, j, ntile, W, seg["off"] + t0_

        def load_x(ti):
            seg, j, _, W, col = tinfo(ti)
            self.ld(xr[:, :, 0:W], self.X1[:, :, col:col + W].rearrange("k p t -> p k t"), xr)

        def load_w(g_):
            wgb = wgrp.next()
            self.ld(wgb[:].rearrange("p a k j -> p (a k j)"), self.wugb[l, g_], wgb)
            wq.append(wgb)

        def do_norm(ti, presquared=False):
            seg, j, _, W, col = tinfo(ti)
            g = seg["grp"]
            h2 = h2r.next()
            self.norm_mod(xr, W, lambda k: self.s2[:, l, g, k:k + 1], lambda k: self.mod[:, l, g, 24 + k:25 + k],
                          h2, sq, rstd, tmpr, presquared=presquared)
            return h2

        def do_square(ti):
            seg, j, _, W, col = tinfo(ti)
            self.act(sq[:, :, 0:W], xr[:, :, 0:W], AF.Square, [xr], [sq])

        load_x(0)
        for g_ in range(PF):
            load_w(g_)
        for c in range(0, NFC, 2):
            self.ld(wd[:, c:c + 2, :], self.wdnb[l, :, c:c + 2, :], wd)
        h2 = do_norm(0)
        hcur = 0
        for ti in range(len(tiles)):
            seg, j, ntile, W, col = tinfo(ti)
            g = seg["grp"]
            n = seg["n"]
            cwbase[0] = PV_CWS if g == 1 else PV_CW
            cwl = cwls[g]
            if j == 0:
                self.memset("pool", halos[hcur][:], 0.0, halos[hcur])
            halo, hnew = halos[hcur], halos[1 - hcur]
            self.tt("pool", ht2[:], halo[:, :, :, 1], cwl[:, 1], ALU.mult, [halo, self.pv], [ht2])
            self.tt("pool", hterm[:, :, :, 0], halo[:, :, :, 0], cwl[:, 0], ALU.mult, [halo, self.pv], [hterm])
            self.tt("pool", hterm[:, :, :, 0], hterm[:, :, :, 0], ht2[:], ALU.add, [hterm, ht2], [hterm])
            self.tt("pool", hterm[:, :, :, 1], halo[:, :, :, 1], cwl[:, 0], ALU.mult, [halo, self.pv], [hterm])
            if ti + 1 < len(tiles):
                load_x(ti + 1)
            h2n = None
            pend = []

            def gate(c_, cvs_, eng_="pool"):
                sg = sgr.next()
                self.act(sg[:, 0:W], cvs_[1][:, 0:W], AF.Silu, [cvs_[1]], [sg])
                self.tt(eng_, actb[c_][:, 0:W], sg[:, 0:W], cvs_[0][:, 0:W], ALU.mult, [sg, cvs_[0]], [actb[c_]])

            for gi in range(NG):
                if gi + PF < NG:
                    load_w(gi + PF)
                wgb = wq.pop(0)
                for cc in range(2):
                    c = gi * 2 + cc
                    pss_, cvs = [], []
                    for ag in range(2):
                        ps = self.psr.next()
                        self.mm(ps[:, 0:W], [(wgb[:, ag, k, cc * 128:(cc + 1) * 128], h2[:, k, 0:W]) for k in range(8)], ps, [wgb, h2])
                        pss_.append(ps)
                    for ag in range(2):
                        ci = ag * NFC + c
                        cv = cvr.next()
                        self.act(cv[:, 0:W], pss_[ag][:, 0:W], AF.Identity, [pss_[ag], self.pv], [cv], bias=cbv(ci), scale=cwv(2, ci))
                        cvs.append(cv)
                    for ag in range(2):
                        self.cp("act", hnew[:, ag, c, :], pss_[ag][:, W - 2:W], [pss_[ag]], [hnew])
                    for ag in range(2):
                        ci = ag * NFC + c
                        self.stt("dve", cvs[ag][:, 1:W], pss_[ag][:, 0:W - 1], cwv(1, ci), cvs[ag][:, 1:W], ALU.mult, ALU.add,
                                 [pss_[ag], cvs[ag], self.pv], [cvs[ag]])
                    for ag in range(2):
                        ci = ag * NFC + c
                        self.stt("dve", cvs[ag][:, 2:W], pss_[ag][:, 0:W - 2], cwv(0, ci), cvs[ag][:, 2:W], ALU.mult, ALU.add,
                                 [pss_[ag], cvs[ag], self.pv], [cvs[ag]])
                    for ag in range(2):
                        self.tt("pool", cvs[ag][:, 0:2], cvs[ag][:, 0:2], hterm[:, ag, c, :], ALU.add, [cvs[ag], hterm], [cvs[ag]])
                    pend.append((c, cvs))
                    if len(pend) > 2:
                        gate(*pend.pop(0))
                if gi == 3 and ti + 1 < len(tiles):
                    do_square(ti + 1)
                if gi == 7 and ti + 1 < len(tiles):
                    h2n = do_norm(ti + 1, presquared=True)
            while pend:
                gate(*pend.pop(0), eng_="dve")
            lo = 1 if j == 0 else 0
            self.ld(xsb[:, :, lo:W], self.X1[:, :, col - 1 + lo:col - 1 + W].rearrange("k p t -> p k t"), xsb)
            if ti + 1 < len(tiles):
                for g_ in range(PF):
                    load_w(g_)
            for m in range(8):
                ps = self.psr.next()
                for c in range(NFC):
                    self.mm(ps[:, 0:W], [(wd[:, c, m * 128:(m + 1) * 128], actb[c][:, 0:W])], ps, [wd, actb[c]],
                            start=(c == 0), stop=(c == NFC - 1), signal=(c == NFC - 1))
                xo = xor_.next()
                self.stt("dve", xo[:, lo:W], ps[:, lo:W], self.mod[:, l, g, 40 + m:41 + m], xsb[:, m, lo:W],
                         ALU.mult, ALU.add, [ps, xsb, self.mod], [xo])
                self.st(self.X0[m, :, col - 1 + lo:col - 1 + W], xo[:, lo:W], xo)
            hcur = 1 - hcur
            if j == ntile - 1:
                halo = halos[hcur]
                cend = seg["off"] + n - 1
                self.tt("dve", fin[:], halo[:, :, :, 0], cwl[:, 0], ALU.mult, [halo, self.pv], [fin])
                self.tt("dve", fin2[:], halo[:, :, :, 1], cwl[:, 1], ALU.mult, [halo, self.pv], [fin2])
                self.tt("dve", fin[:], fin[:], fin2[:], ALU.add, [fin, fin2], [fin])
                self.tt("dve", fin[:], fin[:], cbl, ALU.add, [fin, self.pv], [fin])
                self.act(fin2[:, 1, :], fin[:, 1, :], AF.Silu, [fin], [fin2])
                self.memset("pool", actf[:], 0.0, actf)
                self.tt("dve", actf[:, :, 0], fin2[:, 1, :], fin[:, 0, :], ALU.mult, [fin, fin2], [actf])
                self.ld(xf[:, :, 0:1], self.X1[:, :, cend:cend + 1].rearrange("k p t -> p k t"), xf)
                for m in range(8):
                    ps = self.psr.next()
                    self.mm(ps[:, 0:2], [(wd[:, c, m * 128:(m + 1) * 128], actf[:, c, :]) for c in range(NFC)], ps, [wd, actf])
                    self.stt("dve", xfo[:, m, 0:1], ps[:, 0:1], self.mod[:, l, g, 40 + m:41 + m], xf[:, m, 0:1],
                             ALU.mult, ALU.add, [ps, xf, self.mod], [xfo])
                self.st(self.X0[:, :, cend:cend + 1].rearrange("k p t -> p k t"), xfo[:, :, 0:1], xfo)
            h2 = h2n

    def epilogue(self):
        self.phase_begin()
        xr = self.sb([128, 8, 512], F32, n=2)
        sq = self.sb([128, 8, 512], BF16)
        yT = self.sb([128, 8, 512], F32, n=2)
        rstd = self.sb([128, 512], F32)
        yo = self.sb([128, 4, D], F32, n=2)
        tl = []
        for seg in SEGS:
            nout = NQH if seg["pb"] is None else seg["n"]
            for t0 in range(0, nout, 512):
                tl.append((seg, t0, min(512, nout - t0)))

        def pre_x(i):
            seg_, t0_, W_ = tl[i]
            xt_ = xr.next()
            c_ = seg_["off"] + t0_
            self.ld(xt_[:, :, 0:W_], self.X0[:, :, c_:c_ + W_].rearrange("k p t -> p k t"), xt_)
            return xt_

        xt_next = pre_x(0)
        for ti, (seg, t0, W) in enumerate(tl):
            dst = self.y_s if seg["pb"] is None else self.y_p[seg["pb"]]
            nb = W // 128
            xt = xt_next
            if ti + 1 < len(tl):
                xt_next = pre_x(ti + 1)
            self.act(sq[:, :, 0:W], xt[:, :, 0:W], AF.Square, [xt], [sq])
            ss = self.psr.next()
            self.mm(ss[:, 0:W], [(self.ones[:], sq[:, k, 0:W]) for k in range(8)], ss, [self.ones, sq])
            self._obuf = rstd
            self.rsqrt_ps(rstd[:, 0:W], ss[:, 0:W], ss, 1.0 / D)
            y = yT.next()
            for k in range(8):
                self.stt("dve", y[:, k, 0:W], xt[:, k, 0:W], self.pv[:, PV_GFIN + k:PV_GFIN + k + 1], rstd[:, 0:W],
                         ALU.mult, ALU.mult, [xt, rstd, self.pv], [y])
            yy = yo.next()
            for b_ in range(nb):
                for k4 in range(2):
                    ps = self.psr.next()
                    for kk in range(4):
                        k = k4 * 4 + kk
                        self.transpose(ps[:, kk * 128:(kk + 1) * 128], y[:, k, b_ * 128:(b_ + 1) * 128], ps, [y])
                    self.cp("act" if k4 else "dve", yy[:, b_, k4 * 512:(k4 + 1) * 512], ps[:, :], [ps], [yy])
            self.st(dst[t0:t0 + W, :].rearrange("(b p) f -> p b f", p=128), yy[:, 0:nb, :], yy)

    def build(self):
        self.setup_persistent()
        self.phase_begin()
        self.prologue()
        self.prologue2()
        stages = []
        for l in range(L):
            stages += [("a", l), ("ba", l), ("bc", l), ("bf", l), ("bo", l), ("c", l)]
        for nm, l in stages:
            if self.stop is not None and (nm, l) == self.stop:
                break
            getattr(self, "phase_" + nm)(l)
        else:
            self.epilogue()
        self.fw.finish()
        return self.nc


PV_GMIX = 0
PV_GFFN = PV_GMIX + L * 8
PV_GFIN = PV_GFFN + L * 8
PV_BADA = PV_GFIN + 8
PV_GCQ = PV_BADA + L * 48
PV_GCKV = PV_GCQ + L * 2
PV_SINK = PV_GCKV + L
PV_CW = PV_SINK + L * 8
PV_CB = PV_CW + L * 132
PV_C = PV_CB + L * 44
PV_CWS = PV_C + 16
PV_N = PV_CWS + L * 132


def _perm_partner(dim):
    q = dim // 4
    idx = np.arange(dim).reshape(2, 2, q)
    return idx[:, ::-1, :].reshape(dim)


def _rope_tables(dim, rows_used, row_off, n):
    q = dim // 4
    inv = (10000.0 ** (-np.arange(q, dtype=np.float32) / q)).astype(np.float32)
    t = np.arange(n)
    r = (t // 64).astype(np.float32)
    c = (t % 64).astype(np.float32)
    tab = np.zeros((2, 128, n), np.float32)
    tab[0] = 1.0
    for p in range(rows_used):
        d = p % dim
        ax = r if d < dim // 2 else c
        dd = d % (dim // 2)
        ang = (ax * inv[dd % q]).astype(np.float32)
        tab[0, row_off + p] = np.cos(ang)
        tab[1, row_off + p] = np.sin(ang) * (-1.0 if dd < q else 1.0)
    return tab


_CONST = {}


def _consts():
    if _CONST:
        return _CONST
    bf = ml_dtypes.bfloat16
    _CONST["ropeA"] = _rope_tables(64, 128, 0, NS)
    _CONST["ropeC"] = _rope_tables(32, 32, 64, NS)
    b = np.arange(128)[:, None, None]
    rel = np.arange(6)[None, :, None]
    qi = np.arange(512)[None, None, :]
    d = qi - (rel - 1) * 128 - b
    _CONST["maskc"] = np.where(np.abs(d) <= 128, 0.0, -30000.0).astype(np.float32).astype(bf)
    j = np.arange(64)
    ang = 2 * np.pi * np.outer(j, j) / 64
    C64, S64 = np.cos(ang), np.sin(ang)
    d64 = np.zeros((128, 256), np.float64)
    for g in range(2):
        d64[g * 64:(g + 1) * 64, g * 64:(g + 1) * 64] = C64
        d64[g * 64:(g + 1) * 64, 128 + g * 64:128 + (g + 1) * 64] = -S64
    _CONST["dft64"] = d64.astype(np.float32).astype(bf)
    n = np.arange(NP)
    nk = (np.outer(n, n) % NP).astype(np.float64)
    a = 2 * np.pi * nk / NP
    dp = np.stack([np.cos(a), np.sin(a)]).reshape(2, 2, 128, NP).transpose(0, 2, 1, 3)
    _CONST["dftp"] = np.ascontiguousarray(dp).astype(np.float32).astype(bf)
    n = np.arange(NS, dtype=np.int64)
    nk = (np.outer(n, n) % NS)
    tabc = np.cos(2 * np.pi * np.arange(NS) / NS).astype(np.float32).astype(bf)
    tabs = np.sin(2 * np.pi * np.arange(NS) / NS).astype(np.float32).astype(bf)
    out = np.empty((2, 8, 128, 32, 512), bf)
    for i, tab in enumerate((tabc, tabs)):
        m = tab[nk]
        out[i] = m.reshape(32, 128, 8, 512).transpose(2, 1, 0, 3)
    _CONST["dfts"] = out
    _CONST["ropeA_r"] = np.ascontiguousarray(_CONST["ropeA"][:, :, ::-1])
    _CONST["ropeC_r"] = np.ascontiguousarray(_CONST["ropeC"][:, :, ::-1])
    n1 = np.arange(1, NS + 1, dtype=np.int64)
    nk1 = (np.outer(n1, n1) % NS)
    outr = np.empty((2, 8, 128, 32, 512), bf)
    for i, tab in enumerate((tabc, tabs)):
        m = tab[nk1]
        outr[i] = m.reshape(32, 128, 8, 512).transpose(2, 1, 0, 3)
    _CONST["dfts_r"] = outr
    return _CONST


def _layout_weights(w_in, w_uq, w_ukv, w_out, w_ug, w_down, w_ada):
    f = np.float32
    pa = _perm_partner(64)
    pc = _perm_partner(32)
    w1 = np.empty((L, D, C1), f)
    qa = w_in[:, :, 0:512]
    w1[:, :, C_QA:C_QA + 512] = qa
    w1[:, :, C_QAP:C_QAP + 512] = qa.reshape(L, D, 8, 64)[:, :, :, pa].reshape(L, D, 512)
    ka = w_in[:, :, 512:640].reshape(L, D, 2, 64)
    kap = ka[:, :, :, pa]
    for kv in range(2):
        w1[:, :, C_KD0 + kv * 128:C_KD0 + kv * 128 + 64] = ka[:, :, kv]
        w1[:, :, C_KD0 + kv * 128 + 64:C_KD0 + kv * 128 + 128] = ka[:, :, kv]
        w1[:, :, C_KD0P + kv * 128:C_KD0P + kv * 128 + 64] = kap[:, :, kv]
        w1[:, :, C_KD0P + kv * 128 + 64:C_KD0P + kv * 128 + 128] = kap[:, :, kv]
    w1[:, :, C_KAN:C_KAN + 128] = w_in[:, :, 512:640]
    w1[:, :, C_VA:C_VA + 128] = w_in[:, :, 640:768]
    w1[:, :, C_FB:C_FB + 256] = w_in[:, :, 768:1024]
    w1[:, :, C_CQ:C_CQ + 192] = w_in[:, :, 1024:1216]
    w1[:, :, C_CKV:C_CKV + 128] = w_in[:, :, 1216:1344]
    w1[:, :, C_KR:C_KR + 64] = w_in[:, :, 1280:1344]
    w1[:, :, C_KR + 64:C_KR + 96] = w_in[:, :, 1344:1376]
    w1[:, :, C_KRP:C_KRP + 64] = w_in[:, :, 1280:1344]
    w1[:, :, C_KRP + 64:C_KRP + 96] = w_in[:, :, 1344:1376][:, :, pc]
    w1 = np.ascontiguousarray(w1.reshape(L, 8, 128, C1).transpose(0, 2, 1, 3))
    uq = np.zeros((L, 256, 768), f)
    uq[:, 0:192, 0:384] = w_uq
    uqp = w_uq.reshape(L, 192, 4, 96).copy()
    uqp[:, :, :, 64:96] = uqp[:, :, :, 64:96][:, :, :, pc]
    uq[:, 0:192, 384:768] = uqp.reshape(L, 192, 384)
    uq = np.ascontiguousarray(uq.reshape(L, 2, 128, 768).transpose(0, 2, 1, 3))
    kvw = w_ukv.reshape(L, 128, 4, 128)
    ukv = np.ascontiguousarray(np.concatenate([kvw[:, :, :, 0:64].reshape(L, 128, 256),
                                               kvw[:, :, :, 64:128].reshape(L, 128, 256)], axis=2))
    wo = np.ascontiguousarray(w_out.reshape(L, 8, 128, D).transpose(0, 2, 1, 3))
    wug = np.ascontiguousarray(w_ug.reshape(L, 8, 128, 2, 11, 256).transpose(0, 4, 2, 3, 1, 5)).reshape(L, 11, 128, 2 * 8 * 256)
    wdn = np.ascontiguousarray(w_down.reshape(L, NFC, 128, D).transpose(0, 2, 1, 3))
    wad = np.ascontiguousarray(w_ada.reshape(L, 8, 128, 6 * D).transpose(0, 2, 1, 3))
    return dict(w1=w1, wuq=uq, wukv=ukv, wout=wo, wug=wug, wdn=wdn, wada=wad)


def _pvec(c_vec, c_ctx, b_ada, g_mix, sink, g_cq, g_ckv, g_ffn, conv_w, conv_b, g_final, rev):
    pv = np.zeros((128, PV_N), np.float32)
    pv[:, PV_GMIX:PV_GMIX + L * 8] = g_mix.reshape(L, 8, 128).transpose(2, 0, 1).reshape(128, L * 8)
    pv[:, PV_GFFN:PV_GFFN + L * 8] = g_ffn.reshape(L, 8, 128).transpose(2, 0, 1).reshape(128, L * 8)
    pv[:, PV_GFIN:PV_GFIN + 8] = g_final.reshape(8, 128).T
    pv[:, PV_BADA:PV_BADA + L * 48] = b_ada.reshape(L, 48, 128).transpose(2, 0, 1).reshape(128, L * 48)
    gq = np.zeros((L, 256), np.float32)
    gq[:, 0:192] = g_cq
    pv[:, PV_GCQ:PV_GCQ + L * 2] = gq.reshape(L, 2, 128).transpose(2, 0, 1).reshape(128, L * 2)
    pv[:, PV_GCKV:PV_GCKV + L] = g_ckv.T
    pv[:, PV_SINK:PV_SINK + L * 8] = sink.reshape(1, L * 8)
    pv[:, PV_CW:PV_CW + L * 132] = conv_w.reshape(L, 3, 44, 128).transpose(3, 0, 1, 2).reshape(128, L * 132)
    cws = conv_w[:, ::-1, :] if rev else conv_w
    pv[:, PV_CWS:PV_CWS + L * 132] = cws.reshape(L, 3, 44, 128).transpose(3, 0, 1, 2).reshape(128, L * 132)
    pv[:, PV_CB:PV_CB + L * 44] = conv_b.reshape(L, 44, 128).transpose(2, 0, 1).reshape(128, L * 44)
    cc = np.stack([c_ctx.reshape(8, 128).T, c_vec.reshape(8, 128).T], axis=2)
    pv[:, PV_C:PV_C + 16] = cc.reshape(128, 16)
    return pv


_PROG = {}


def _get_prog(debug=False, stop=None):
    key = (debug, stop)
    if key not in _PROG:
        _PROG[key] = Prog(debug=debug, stop=stop).build()
    return _PROG[key]


def make_in_maps(x_prompt, x_sample, cache_win_k, cache_win_v, cache_mla_ckv, cache_mla_krope,
                 c, c_ctx, w_ada, b_ada, g_mix, w_in, sink, g_cq, w_uq, g_ckv, w_ukv, w_out,
                 g_ffn, w_ug, conv_w, conv_b, w_down, g_final, cores=range(8)):
    A = lambda a: np.ascontiguousarray(np.asarray(a), dtype=np.float32)
    ws = _layout_weights(A(w_in), A(w_uq), A(w_ukv), A(w_out), A(w_ug), A(w_down), A(w_ada))
    cst = _consts()
    x_prompt, x_sample = A(x_prompt), A(x_sample)
    cwk, cwv, cckv, ckr = A(cache_win_k), A(cache_win_v), A(cache_mla_ckv), A(cache_mla_krope)
    c, c_ctx = A(c), A(c_ctx)
    maps = []
    for i in cores:
        b = i // 2
        rev = (i % 2 == 1)
        m = dict(ws)
        for k_ in ("maskc", "dft64", "dftp"):
            m[k_] = cst[k_]
        for k_ in ("ropeA", "ropeC", "dfts"):
            m[k_] = cst[k_ + "_r"] if rev else cst[k_]
        m["x_s"] = np.ascontiguousarray(x_sample[b][::-1]) if rev else x_sample[b]
        m["x_p"] = x_prompt[2 * i:2 * i + 2]
        m["c_wk"] = cwk[b].reshape(L, PAST, 128)
        m["c_wv"] = cwv[b].reshape(L, PAST, 128)
        m["c_ckv"] = cckv[b]
        m["c_kr"] = ckr[b]
        m["pvec"] = _pvec(c[b], c_ctx, A(b_ada), A(g_mix), A(sink), A(g_cq), A(g_ckv), A(g_ffn), A(conv_w), A(conv_b), A(g_final), rev)
        maps.append(m)
    return maps


def kernel(**inputs):
    nc = _get_prog()
    maps = make_in_maps(**inputs)
    res = run_bass_kernel_spmd(nc, maps, core_ids=list(range(8))).results
    y_prompt = np.concatenate([res[i]["y_p"] for i in range(8)], axis=0).astype(np.float32)
    y_sample = np.stack([np.concatenate([res[2 * b]["y_s"], res[2 * b + 1]["y_s"][::-1]], axis=0)
                         for b in range(4)], axis=0).astype(np.float32)
    cat = lambda k: np.concatenate([res[i][k] for i in range(8)], axis=0).astype(np.float32)
    swk = cat("o_wk").reshape(16, L, NP, 2, 64)
    swv = cat("o_wv").reshape(16, L, NP, 2, 64)
    sckv = cat("o_ckv").reshape(16, L, NP, 128)
    skr = cat("o_kr").reshape(16, L, NP, 32)
    return (y_prompt, y_sample, swk, swv, sckv, skr)
```

```python
import os
import numpy as np
import ml_dtypes
import concourse.bass as bass
import concourse.mybir as mybir
from concourse.bass_utils import run_bass_kernel_spmd

F32 = mybir.dt.float32
BF16 = mybir.dt.bfloat16
ALU = mybir.AluOpType
AF = mybir.ActivationFunctionType

L = 2
D = 1024
NS = 4096
NP = 256
NT = NS + 2 * NP
PAST = 256
NK = NS + PAST + 2 * NP
DFF = 2816
NFC = 22
EPS = 1e-6
C_QA, C_QAP = 0, 512
C_KD0, C_KD1, C_KD0P, C_KD1P = 1024, 1152, 1280, 1408
C_KAN, C_VA, C_FB, C_CQ, C_CKV, C_KR, C_KRP = 1536, 1664, 1792, 2048, 2240, 2368, 2464
C1 = 2560

NQH = NS // 2
NQL = NQH + 128


def qtiles(seg, l):
    if seg["pb"] is None and l == L - 1:
        return [(t0, min(512, NQL - t0)) for t0 in range(0, NQL, 512)]
    return [(t0, min(512, seg["n"] - t0)) for t0 in range(0, seg["n"], 512)]


SEGS = [
    dict(name="s", off=0, n=NS, koff=0, nk=NS + PAST, grp=1, rope=True, pb=None),
    dict(name="p0", off=NS, n=NP, koff=NS + PAST, nk=NP, grp=0, rope=False, pb=0),
    dict(name="p1", off=NS + NP, n=NP, koff=NS + PAST + NP, nk=NP, grp=0, rope=False, pb=1),
]


class TU:
    __slots__ = ("w", "r")

    def __init__(self):
        self.w = {}
        self.r = {}


class Buf:
    def __init__(self, t, excl=False):
        self.t = t
        self.tu = TU()
        self.excl = excl

    def __getitem__(self, idx):
        return self.t[idx]


class BankView:
    def __init__(self, t, b, n):
        self.t, self.b, self.n = t, b, n

    def __getitem__(self, idx):
        p, f = idx
        if self.n == 1:
            return self.t[p, self.b, f]
        return self.t[p, self.b:self.b + self.n, f]


class Ring:
    def __init__(self, bufs):
        self.bufs = bufs
        self.i = 0

    def next(self):
        b = self.bufs[self.i]
        self.i = (self.i + 1) % len(self.bufs)
        return b


class Eng:
    def __init__(self, name, sem):
        self.name = name
        self.sem = sem
        self.count = 0
        self.waited = {}
        self.thunks = []
        self.dma_pool = []
        self.dma_next = 0


class FW:
    def __init__(self, nc, n_dma_sems=16):
        self.nc = nc
        self.eng = {}
        for nm in ("pe", "act", "dve", "pool", "sp"):
            self.eng[nm] = Eng(nm, nc.alloc_semaphore(name=f"prog_{nm}"))
        for nm in ("sp", "pool"):
            e = self.eng[nm]
            for i in range(n_dma_sems):
                e.dma_pool.append([nc.alloc_semaphore(name=f"dma_{nm}_{i}"), 0])
        self.n_inst = 0
        self.bg = {}
        self.bg_all = []

    def dma_bg(self, eng, out, in_, tag):
        e = self.eng[eng]
        sem = self.nc.alloc_semaphore(name=f"bg_{len(self.bg_all)}")
        e.thunks.append(lambda h, out=out, in_=in_, sem=sem: h.dma_start(out=out, in_=in_).then_inc(sem, 16))
        self.bg.setdefault(tag, []).append((sem, 16))
        self.bg_all.append((sem, 16))

    def wait_bg(self, eng, tag):
        e = self.eng[eng]
        for sem, val in self.bg.get(tag, []):
            self._wait(e, sem, val)

    @staticmethod
    def _merge(deps, d):
        for s, v in d.items():
            if deps.get(s, (None, 0))[1] < v[1]:
                deps[s] = v

    def _deps(self, reads, writes, own=None):
        deps = {}
        for b in reads:
            self._merge(deps, b.tu.w)
            if b.excl:
                self._merge(deps, {k: v for k, v in b.tu.r.items() if v[0] is not own})
        for b in writes:
            self._merge(deps, b.tu.w)
            self._merge(deps, b.tu.r)
        return deps

    def _wait(self, e, sem, val):
        if e.waited.get(id(sem), 0) < val:
            e.waited[id(sem)] = val
            e.thunks.append(lambda h, sem=sem, val=val: h.wait_ge(sem, val))

    def _emit_waits(self, e, deps, skip_self=False):
        for sid, (sem, val) in deps.items():
            if skip_self and sem is e.sem:
                continue
            self._wait(e, sem, val)

    def _record(self, reads, writes, tok):
        sid = id(tok[0])
        for b in reads:
            if b.tu.r.get(sid, (None, 0))[1] < tok[1]:
                b.tu.r[sid] = tok
        for b in writes:
            b.tu.w = {sid: tok}
            b.tu.r = {}

    def op(self, eng, fn, reads=(), writes=(), signal=True):
        e = self.eng[eng]
        assert signal or eng == "pe"
        self._emit_waits(e, self._deps(reads, writes, own=e.sem), skip_self=(eng == "pe"))
        if signal:
            e.count += 1
            sem = e.sem
            e.thunks.append(lambda h, fn=fn, sem=sem: fn(h).then_inc(sem, 1))
            tok = (e.sem, e.count)
        else:
            e.thunks.append(lambda h, fn=fn: fn(h))
            tok = (e.sem, e.count + 1)
        self._record(reads, writes, tok)
        self.n_inst += 1

    def dma(self, eng, out, in_, reads=(), writes=()):
        e = self.eng[eng]
        self._emit_waits(e, self._deps(reads, writes))
        slot = e.dma_pool[e.dma_next]
        e.dma_next = (e.dma_next + 1) % len(e.dma_pool)
        sem = slot[0]
        if slot[1] > 0:
            self._wait(e, sem, slot[1] * 16)
        slot[1] += 1
        e.thunks.append(lambda h, out=out, in_=in_, sem=sem: h.dma_start(out=out, in_=in_).then_inc(sem, 16))
        self._record(reads, writes, (sem, slot[1] * 16))
        self.n_inst += 1

    def barrier(self):
        for e in self.eng.values():
            for f in self.eng.values():
                if f is not e and f.count > 0:
                    self._wait(e, f.sem, f.count)
                for sem, cnt in f.dma_pool:
                    if cnt > 0:
                        self._wait(e, sem, cnt * 16)

    def finish(self):
        for sem, val in self.bg_all:
            self._wait(self.eng["sp"], sem, val)
        self.barrier()
        nc = self.nc
        with nc.allow_non_contiguous_dma(reason="small strided state/param transfers"):
            with nc.Block() as block:
                @block.tensor
                def _(h):
                    for th in self.eng["pe"].thunks:
                        th(h)

                @block.scalar
                def _(h):
                    for th in self.eng["act"].thunks:
                        th(h)

                @block.vector
                def _(h):
                    for th in self.eng["dve"].thunks:
                        th(h)

                @block.gpsimd
                def _(h):
                    for th in self.eng["pool"].thunks:
                        th(h)

                @block.sync
                def _(h):
                    for th in self.eng["sp"].thunks:
                        th(h)


class Prog:
    def __init__(self, debug=False, stop=None):
        self.debug = debug
        self.stop = stop
        nc = self.nc = bass.Bass("TRN2", target_bir_lowering=False)
        self.fw = FW(nc)
        self.sb_lo = (nc.sbuf_base + 63) // 64 * 64
        self.sb_hi = nc.sbuf_top
        self.sb_ptr = self.sb_lo
        self.nalloc = 0
        self.decl_io()
        self.psall = nc.alloc_psum_tensor("psall", [128, 8, 512], F32)
        self.psum = [Buf(BankView(self.psall, i, 1), excl=True) for i in range(8)]
        self.psr = Ring(self.psum)
        self.pso = Ring(self.psum[0:2])
        self.pss = Ring(self.psum[2:8])

    def sb(self, shape, dt, n=1):
        bufs = []
        per = int(np.prod(shape[1:])) * (4 if dt == F32 else 2)
        per = (per + 63) // 64 * 64
        for _ in range(n):
            assert self.sb_ptr + per <= self.sb_hi, ("SBUF overflow", self.sb_ptr, per, self.sb_hi)
            t = self.nc.alloc_sbuf_tensor_at(f"sb{self.nalloc}", list(shape), dt, offset=self.sb_ptr)
            self.nalloc += 1
            self.sb_ptr += per
            bufs.append(Buf(t))
        return bufs[0] if n == 1 else Ring(bufs)

    def phase_begin(self):
        self.fw.barrier()
        self.sb_ptr = self.sb_persist
        for b in self.psum:
            b.tu = TU()

    def din(self, name, shape, dt=F32):
        return self.nc.dram_tensor(name, list(shape), dt, kind="ExternalInput").ap()

    def dout(self, name, shape, dt=F32):
        return self.nc.dram_tensor(name, list(shape), dt, kind="ExternalOutput").ap()

    def dscr(self, name, shape, dt):
        if self.debug:
            return self.nc.dram_tensor(name, list(shape), dt, kind="ExternalOutput").ap()
        return self.nc.dram_tensor(name, list(shape), dt).ap()

    def decl_io(self):
        self.xs_in = self.din("x_s", [NS, D])
        self.xp_in = self.din("x_p", [2, NP, D])
        self.cwk = self.din("c_wk", [L, PAST, 128])
        self.cwv = self.din("c_wv", [L, PAST, 128])
        self.cckv = self.din("c_ckv", [L, PAST, 128])
        self.ckr = self.din("c_kr", [L, PAST, 32])
        self.pvec = self.din("pvec", [128, PV_N])
        self.w1 = self.din("w1", [L, 128, 8, C1])
        self.wuq = self.din("wuq", [L, 128, 2, 768])
        self.wukv = self.din("wukv", [L, 128, 512])
        self.wout = self.din("wout", [L, 128, 8, D])
        self.wug = self.din("wug", [L, 11, 128, 2 * 8 * 256])
        self.wdn = self.din("wdn", [L, 128, NFC, D])
        self.wada = self.din("wada", [L, 128, 8, 6 * D])
        self.ropeA = self.din("ropeA", [2, 128, NS])
        self.ropeC = self.din("ropeC", [2, 128, NS])
        self.maskc = self.din("maskc", [128, 6, 512], BF16)
        self.dft64 = self.din("dft64", [128, 256], BF16)
        self.dftp = self.din("dftp", [2, 128, 2, 256], BF16)
        self.dfts = self.din("dfts", [2, 8, 128, 32, 512], BF16)
        self.y_s = self.dout("y_s", [NQH, D])
        self.y_p = self.dout("y_p", [2, NP, D])
        self.o_wk = self.dout("o_wk", [2, L, NP, 128])
        self.o_wv = self.dout("o_wv", [2, L, NP, 128])
        self.o_ckv = self.dout("o_ckv", [2, L, NP, 128])
        self.o_kr = self.dout("o_kr", [2, L, NP, 32])
        self.X0 = self.dscr("X0", [8, 128, NT], F32)
        self.X1 = self.dscr("X1", [8, 128, NT], F32)
        self.QA = self.dscr("QA", [4, 128, NT], BF16)
        self.KA = self.dscr("KA", [2, 128, NK], BF16)
        self.VA = self.dscr("VA", [NK // 128, 128, 512], BF16)
        self.QC = self.dscr("QC", [4, 128, NT], BF16)
        self.KC = self.dscr("KC", [4, 128, NK], BF16)
        self.VC = self.dscr("VC", [NK // 128, 128, 512], BF16)
        self.Z = self.dscr("Z", [NT // 128, 128, 512], BF16)
        self.MIX = self.dscr("MIX", [8, 128, NT], BF16)
        self.w1b = self.dscr("w1b", [L, 128, 8, C1], BF16)
        self.wuqb = self.dscr("wuqb", [L, 128, 2, 768], BF16)
        self.wukvb = self.dscr("wukvb", [L, 128, 512], BF16)
        self.woutb = self.dscr("woutb", [L, 128, 8, D], BF16)
        self.wugb = self.dscr("wugb", [L, 11, 128, 2 * 8 * 256], BF16)
        self.wdnb = self.dscr("wdnb", [L, 128, NFC, D], BF16)

    def mm(self, out, pairs, pbuf, reads, start=True, stop=True, signal=True):
        n = len(pairs)
        for i, (lhsT, rhs) in enumerate(pairs):
            st = start and i == 0
            sp = stop and i == n - 1
            self.fw.op("pe", lambda h, o=out, a=lhsT, b=rhs, st=st, sp=sp: h.matmul(o, lhsT=a, rhs=b, start=st, stop=sp),
                       reads=reads, writes=[pbuf], signal=(signal and i == n - 1))

    def act(self, out, in_, func, reads, writes, bias=0.0, scale=1.0):
        self.fw.op("act", lambda h: h.activation(out=out, in_=in_, func=func, bias=bias, scale=scale),
                   reads=reads, writes=writes)

    def ts(self, eng, out, in0, s1, s2, op0, op1, reads, writes):
        if s2 is None:
            self.fw.op(eng, lambda h: h.tensor_scalar(out=out, in0=in0, scalar1=s1, scalar2=None, op0=op0),
                       reads=reads, writes=writes)
        else:
            self.fw.op(eng, lambda h: h.tensor_scalar(out=out, in0=in0, scalar1=s1, scalar2=s2, op0=op0, op1=op1),
                       reads=reads, writes=writes)

    def stt(self, eng, out, in0, sc, in1, op0, op1, reads, writes):
        self.fw.op(eng, lambda h: h.scalar_tensor_tensor(out=out, in0=in0, scalar=sc, in1=in1, op0=op0, op1=op1),
                   reads=reads, writes=writes)

    def tt(self, eng, out, in0, in1, op, reads, writes):
        self.fw.op(eng, lambda h: h.tensor_tensor(out=out, in0=in0, in1=in1, op=op), reads=reads, writes=writes)

    def cp(self, eng, out, in_, reads, writes):
        if eng == "act":
            self.fw.op("act", lambda h: h.activation(out=out, in_=in_, func=AF.Copy), reads=reads, writes=writes)
        else:
            self.fw.op(eng, lambda h: h.tensor_copy(out, in_), reads=reads, writes=writes)

    def memset(self, eng, ap, val, buf):
        self.fw.op(eng, lambda h: h.memset(ap, val), writes=[buf])

    def rsqrt_ps(self, out, ps_ap, ps_buf, scale, rows=slice(0, 128)):
        self.act(out, ps_ap, AF.Sqrt, [ps_buf], [self._obuf], bias=self.epsb[rows, 0:1], scale=scale)
        self.recip(out, out, [self._obuf], [self._obuf])

    def recip_pool(self, ap, buf, rows, w):
        self.recip(ap, ap, [buf], [buf])

    def recip(self, out, in_, reads, writes):
        self.fw.op("dve", lambda h: h.reciprocal(out=out, in_=in_), reads=reads, writes=writes)

    def ld(self, out, in_, buf, q="sp"):
        self.fw.dma(q, out, in_, writes=[buf])

    def st(self, out, in_, buf, q="sp"):
        self.fw.dma(q, out, in_, reads=[buf])

    def transpose(self, out, in_, pbuf, reads):
        self.fw.op("pe", lambda h: h.transpose(out, in_, self.ident[:]), reads=list(reads) + [self.ident], writes=[pbuf])

    def setup_persistent(self):
        self.ident = self.sb([128, 128], F32)
        self.ones = self.sb([128, 128], BF16)
        self.pv = self.sb([128, PV_N], F32)
        self.mask = self.sb([128, 6, 512], BF16)
        self.d64 = self.sb([128, 256], BF16)
        self.dp = self.sb([128, 2, 2, 256], BF16)
        self.mod = self.sb([128, L, 2, 48], F32)
        self.s1 = self.sb([128, L, 2, 8], F32)
        self.s2 = self.sb([128, L, 2, 8], F32)
        self.esink = self.sb([128, L * 8], F32)
        self.silc = self.sb([128, 8, 2], F32)
        self.epsb = self.sb([128, 1], F32)
        self.identb = self.sb([128, 128], BF16)
        self.neg1 = self.sb([128, 512], F32)
        self.negh = self.sb([128, 512], F32)
        self.sb_persist = self.sb_ptr
        self.memset("pool", self.neg1[:], -1.0, self.neg1)
        self.memset("pool", self.negh[:], -0.5, self.negh)
        self.memset("dve", self.epsb[:], EPS, self.epsb)
        self.memset("pool", self.ident[:], 1.0, self.ident)
        self.fw.op("pool", lambda h: h.affine_select(out=self.ident[:], in_=self.ident[:], pattern=[[-1, 128]],
                                                     compare_op=ALU.is_equal, fill=0.0, base=0, channel_multiplier=1),
                   reads=[self.ident], writes=[self.ident])
        self.memset("dve", self.ones[:], 1.0, self.ones)
        self.cp("dve", self.identb[:], self.ident[:], [self.ident], [self.identb])
        self.ld(self.pv[:], self.pvec, self.pv)
        self.ld(self.mask[:], self.maskc, self.mask)
        self.ld(self.d64[:], self.dft64, self.d64)
        self.ld(self.dp[:], self.dftp.rearrange("c p n f -> p c n f"), self.dp)
        self.act(self.esink[:], self.pv[:, PV_SINK:PV_SINK + L * 8], AF.Exp, [self.pv], [self.esink])
        self.act(self.silc[:], self.pv[:, PV_C:PV_C + 16].rearrange("p (k g) -> p k g", g=2), AF.Silu, [self.pv], [self.silc])

    def pvs(self, off, l, width, idx):
        c = off + l * width + idx
        return self.pv[:, c:c + 1]

    def prologue(self):
        fw = self.fw
        for k in range(0, 8, 2):
            fw.dma("pool", self.w1b[0, :, k:k + 2, :], self.w1[0, :, k:k + 2, :])
        fw.dma("pool", self.wuqb[0], self.wuq[0])
        fw.dma("pool", self.wukvb[0], self.wukv[0])

    def cast_bg(self, stage):
        fw = self.fw
        if stage == 0:
            fw.dma_bg("pool", self.woutb[0], self.wout[0], "out0")
            for g0 in range(0, 11, 3):
                g1 = min(11, g0 + 3)
                fw.dma_bg("pool", self.wugb[0, g0:g1], self.wug[0, g0:g1], "ffn0")
            for c in range(0, NFC, 11):
                fw.dma_bg("pool", self.wdnb[0, :, c:c + 11, :], self.wdn[0, :, c:c + 11, :], "ffn0")
        elif stage == 1:
            fw.wait_bg("pool", "out0")
            fw.wait_bg("pool", "ffn0")
            for k in range(0, 8, 4):
                fw.dma_bg("pool", self.w1b[1, :, k:k + 4, :], self.w1[1, :, k:k + 4, :], "in1")
            fw.dma_bg("pool", self.wuqb[1], self.wuq[1], "in1")
            fw.dma_bg("pool", self.wukvb[1], self.wukv[1], "in1")
            fw.dma_bg("pool", self.woutb[1], self.wout[1], "out1")
        elif stage == 2:
            fw.wait_bg("pool", "in1")
            fw.wait_bg("pool", "out1")
            for g0 in range(0, 11, 3):
                g1 = min(11, g0 + 3)
                fw.dma_bg("pool", self.wugb[1, g0:g1], self.wug[1, g0:g1], "ffn1")
            for c in range(0, NFC, 11):
                fw.dma_bg("pool", self.wdnb[1, :, c:c + 11, :], self.wdn[1, :, c:c + 11, :], "ffn1")

    def prologue2(self):
        fw = self.fw
        wr = self.sb([128, 8, 512], F32, n=2)
        modrow = self.sb([2, 6 * D], F32)
        for l in range(L):
            for blk in range(12):
                wb = wr.next()
                self.ld(wb[:], self.wada[l, :, :, blk * 512:(blk + 1) * 512], wb)
                pr_ = self.psr.next()
                self.mm(pr_[0:2, :], [(self.silc[:, k, :], wb[:, k, :]) for k in range(8)], pr_, [wb, self.silc])
                self.cp("act" if blk % 2 else "dve", modrow[0:2, blk * 512:(blk + 1) * 512], pr_[0:2, :], [pr_], [modrow])
            pm = self.psr.next()
            for q in range(48):
                self.fw.op("pe", lambda h, q=q, pm=pm: h.transpose(pm[:, q * 2:q * 2 + 2], modrow[0:2, q * 128:(q + 1) * 128], self.ident[0:2, 0:2]),
                           reads=[modrow, self.ident], writes=[pm])
            for g in range(2):
                self.tt("dve", self.mod[:, l, g, :], pm[:, 0:96].rearrange("p (q g) -> p q g", g=2)[:, :, g],
                        self.pv[:, PV_BADA + l * 48:PV_BADA + (l + 1) * 48], ALU.add, [pm, self.pv], [self.mod])
                self.stt("dve", self.s1[:, l, g, :], self.mod[:, l, g, 8:16], 1.0,
                         self.pv[:, PV_GMIX + l * 8:PV_GMIX + (l + 1) * 8], ALU.add, ALU.mult, [self.mod, self.pv], [self.s1])
                self.stt("dve", self.s2[:, l, g, :], self.mod[:, l, g, 32:40], 1.0,
                         self.pv[:, PV_GFFN + l * 8:PV_GFFN + (l + 1) * 8], ALU.add, ALU.mult, [self.mod, self.pv], [self.s2])
        if self.debug:
            dm = self.dout("dbg_mod", [128, L * 2 * 48])
            self.st(dm, self.mod[:].rearrange("p l g q -> p (l g q)"), self.mod)
        xin = self.sb([128, 4, D], F32, n=2)
        xo = self.sb([128, 8, 512], F32, n=2)
        for seg in SEGS:
            src = self.xs_in if seg["pb"] is None else self.xp_in[seg["pb"]]
            for t0 in range(0, seg["n"], 512):
                W = min(512, seg["n"] - t0)
                nb = W // 128
                xi = xin.next()
                self.ld(xi[:, 0:nb, :], src[t0:t0 + W, :].rearrange("(b p) f -> p b f", p=128), xi)
                xt = xo.next()
                for k in range(8):
                    ps = self.psr.next()
                    for b in range(nb):
                        self.transpose(ps[:, b * 128:(b + 1) * 128], xi[:, b, k * 128:(k + 1) * 128], ps, [xi])
                    self.cp("dve" if k % 2 else "act", xt[:, k, 0:W], ps[:, 0:W], [ps], [xt])
                self.st(self.X0[:, :, seg["off"] + t0:seg["off"] + t0 + W].rearrange("k p t -> p k t"), xt[:, :, 0:W], xt)

    def norm_mod(self, xt, W, sc, sh, hT, sq, rstd, tmpr, presquared=False):
        if not presquared:
            self.act(sq[:, :, 0:W], xt[:, :, 0:W], AF.Square, [xt], [sq])
        ss = self.psr.next()
        self.mm(ss[:, 0:W], [(self.ones[:], sq[:, k, 0:W]) for k in range(8)], ss, [self.ones, sq])
        self._obuf = rstd
        self.rsqrt_ps(rstd[:, 0:W], ss[:, 0:W], ss, 1.0 / D)
        for k in range(8):
            tmp = tmpr.next()
            self.stt("dve", tmp[:, 0:W], xt[:, k, 0:W], sc(k), rstd[:, 0:W], ALU.mult, ALU.mult, [xt, rstd], [tmp])
            if sh is None:
                self.cp("act", hT[:, k, 0:W], tmp[:, 0:W], [tmp], [hT])
            else:
                self.act(hT[:, k, 0:W], tmp[:, 0:W], AF.Identity, [tmp], [hT], bias=sh(k), scale=1.0)

    def mla_kv(self, ckvn, W, KCt, VCt, wukv):
        nb = W // 128
        for h in range(4):
            ps = self.psr.next()
            self.mm(ps[0:64, 0:W], [(wukv[:, h * 64:(h + 1) * 64], ckvn[:, 0:W])], ps, [wukv, ckvn])
            self.cp("act" if h % 2 else "dve", KCt[0:64, h, 0:W], ps[0:64, 0:W], [ps], [KCt])
        for b2 in range(0, nb, 2):
            ps = self.psr.next()
            for b in range(b2, min(b2 + 2, nb)):
                self.mm(ps[:, (b - b2) * 256:(b - b2 + 1) * 256], [(ckvn[:, b * 128:(b + 1) * 128], wukv[:, 256:512])],
                        ps, [wukv, ckvn])
            n2 = min(2, nb - b2)
            for h in range(4):
                off = 0 if h % 2 == 0 else 64
                self.cp("dve" if h % 2 else "act", VCt[:, b2:b2 + n2, h, off:off + 64],
                        ps[:, 0:n2 * 256].rearrange("p (b f) -> p b f", f=256)[:, :, h * 64:(h + 1) * 64], [ps], [VCt])

    def phase_a(self, l):
        self.phase_begin()
        self.fw.wait_bg("sp", f"in{l}")
        self.psr = Ring(self.psum[3:8])
        try:
            self._phase_a(l)
        finally:
            self.psr = Ring(self.psum)

    def _phase_a(self, l):
        w1 = self.sb([128, 8, C1], BF16)
        wuq = self.sb([128, 2, 768], BF16)
        wukv = self.sb([128, 512], BF16)
        for k in range(8):
            self.ld(w1[:, k, :], self.w1b[l, :, k, :], w1)
        self.ld(wuq[:], self.wuqb[l], wuq)
        self.ld(wukv[:], self.wukvb[l], wukv)
        xr = self.sb([128, 8, 512], F32, n=2)
        sq = self.sb([128, 8, 512], BF16)
        hr = self.sb([128, 8, 512], BF16, n=2)
        tmpr = self.sb([128, 512], F32, n=3)
        rstd = self.sb([128, 512], F32)
        rstq = self.sb([128, 512], F32)
        rstk = self.sb([128, 512], F32)
        rA = self.sb([128, 2, 512], F32)
        rC = self.sb([128, 2, 512], F32)
        QAt = self.sb([128, 4, 512], BF16)
        KAt = self.sb([128, 2, 512], BF16)
        VAt = self.sb([128, 4, 4, 128], BF16)
        QCt = self.sb([128, 4, 512], BF16)
        KCt = self.sb([128, 4, 512], BF16)
        VCt = self.sb([128, 4, 4, 128], BF16)
        Zt = self.sb([128, 4, 512], BF16)
        fbT = self.sb([128, 2, 512], BF16)
        cqn = self.sb([128, 2, 512], BF16)
        sqq = self.sb([128, 2, 512], BF16)
        ckvn = self.sb([128, 512], BF16)
        sqk = self.sb([128, 512], BF16)
        t1r = self.sb([128, 512], F32, n=2)
        t2r = self.sb([128, 512], F32, n=2)
        stF = self.sb([128, 512], F32, n=2)
        stT = self.sb([128, 4, 128], F32, n=2)
        krF = self.sb([128, 512], F32)
        vaF = self.sb([128, 4, 128], F32)
        ctxin = self.sb([128, 2, 128], F32, n=2)
        self.memset("pool", VAt[:], 1.0, VAt)
        self.memset("pool", VCt[:], 1.0, VCt)
        self.memset("dve", krF[:], 0.0, krF)
        self.memset("dve", KCt[:], 0.0, KCt)
        self.memset("dve", QCt[:], 0.0, QCt)

        def proj(col, M, hT, W):
            ps = self.psr.next()
            self.mm(ps[0:M, 0:W], [(w1[:, k, col:col + M], hT[:, k, 0:W]) for k in range(8)], ps, [w1, hT])
            return ps

        def rope(ps_q, ps_p, tab, rows, W, dst_ap, dst_buf, extra_dsts=()):
            t1 = t1r.next()
            t2 = t2r.next()
            self.tt("dve", t1[rows, 0:W], ps_q[rows, 0:W], tab[rows, 0, 0:W], ALU.mult, [ps_q, tab], [t1])
            self.tt("dve", t2[rows, 0:W], ps_p[rows, 0:W], tab[rows, 1, 0:W], ALU.mult, [ps_p, tab], [t2])
            self.tt("pool", dst_ap, t1[rows, 0:W], t2[rows, 0:W], ALU.add, [t1, t2], [dst_buf])
            for d in extra_dsts:
                self.tt("pool", d, t1[rows, 0:W], t2[rows, 0:W], ALU.add, [t1, t2], [dst_buf])

        def state_out(srcF, W, dst_rows_fn, c0, c1):
            nb = W // 128
            ps = self.psr.next()
            for b in range(nb):
                self.transpose(ps[:, b * 128:(b + 1) * 128], srcF[:, b * 128:(b + 1) * 128], ps, [srcF])
            tt_ = stT.next()
            self.cp("dve", tt_[:, 0:nb, :], ps[:, 0:W].rearrange("p (b f) -> p b f", f=128), [ps], [tt_])
            self.st(dst_rows_fn, tt_[:, 0:nb, c0:c1], tt_)

        tl = [(seg, t0, min(512, seg["n"] - t0)) for seg in SEGS for t0 in range(0, seg["n"], 512)]

        def pre_x(i):
            seg_, t0_, W_ = tl[i]
            xt_ = xr.next()
            c_ = seg_["off"] + t0_
            self.ld(xt_[:, :, 0:W_], self.X0[:, :, c_:c_ + W_].rearrange("k p t -> p k t"), xt_)
            return xt_

        def pre_norm(i, xt_, presquared=False):
            seg_, t0_, W_ = tl[i]
            g_ = seg_["grp"]
            h_ = hr.next()
            self.norm_mod(xt_, W_, lambda k: self.s1[:, l, g_, k:k + 1], lambda k: self.mod[:, l, g_, k:k + 1],
                          h_, sq, rstd, tmpr, presquared=presquared)
            return h_

        def pre_square(i, xt_):
            seg_, t0_, W_ = tl[i]
            self.act(sq[:, :, 0:W_], xt_[:, :, 0:W_], AF.Square, [xt_], [sq])

        hT_next = pre_norm(0, pre_x(0))
        xt_next = pre_x(1)
        for ti, (seg, t0, W) in enumerate(tl):
            if True:
                g = seg["grp"]
                nb = W // 128
                col = seg["off"] + t0
                kcol = seg["koff"] + t0
                hT = hT_next
                if seg["rope"]:
                    self.ld(rA[:, :, 0:W], self.ropeA[:, :, t0:t0 + W].rearrange("c p t -> p c t"), rA)
                    self.ld(rC[:, :, 0:W], self.ropeC[:, :, t0:t0 + W].rearrange("c p t -> p c t"), rC)
                need_q = not (seg["pb"] is None and l == L - 1 and t0 >= NQL)
                def proj_h(bank, col_, M_):
                    ps_ = self.psum[bank]
                    self.mm(ps_[0:M_, 0:W], [(w1[:, k, col_:col_ + M_], hT[:, k, 0:W]) for k in range(8)], ps_, [w1, hT])
                    return ps_
                if need_q:
                    pq0 = proj_h(0, C_CQ, 128)
                    pq1 = proj_h(1, C_CQ + 128, 64)
                    self.act(sqq[:, 0, 0:W], pq0[:, 0:W], AF.Square, [pq0], [sqq])
                    self.act(sqq[0:64, 1, 0:W], pq1[0:64, 0:W], AF.Square, [pq1], [sqq])
                pk = proj_h(2, C_CKV, 128)
                self.act(sqk[:, 0:W], pk[:, 0:W], AF.Square, [pk], [sqk])
                if ti + 1 < len(tl):
                    pre_square(ti + 1, xt_next)
                if need_q:
                    for c in range(4):
                        pq = proj(C_QA + c * 128, 128, hT, W)
                        if seg["rope"]:
                            pp = proj(C_QAP + c * 128, 128, hT, W)
                            rope(pq, pp, rA, slice(0, 128), W, QAt[:, c, 0:W], QAt)
                        else:
                            self.cp("act", QAt[:, c, 0:W], pq[:, 0:W], [pq], [QAt])
                    self.st(self.QA[:, :, col:col + W].rearrange("c p t -> p c t"), QAt[:, :, 0:W], QAt)
                if need_q:
                    ss = self.psr.next()
                    self.mm(ss[:, 0:W], [(self.ones[:], sqq[:, 0, 0:W]), (self.ones[0:64, :], sqq[0:64, 1, 0:W])], ss, [self.ones, sqq])
                    self._obuf = rstq
                    self.rsqrt_ps(rstq[:, 0:W], ss[:, 0:W], ss, 1.0 / 192)
                ss = self.psr.next()
                self.mm(ss[:, 0:W], [(self.ones[:], sqk[:, 0:W])], ss, [self.ones, sqk])
                self._obuf = rstk
                self.rsqrt_ps(rstk[:, 0:W], ss[:, 0:W], ss, 1.0 / 128)
                if need_q:
                    self.stt("dve", cqn[:, 0, 0:W], pq0[:, 0:W], self.pvs(PV_GCQ, l, 2, 0), rstq[:, 0:W], ALU.mult, ALU.mult,
                             [pq0, rstq, self.pv], [cqn])
                    self.stt("dve", cqn[0:64, 1, 0:W], pq1[0:64, 0:W], self.pv[0:64, PV_GCQ + l * 2 + 1:PV_GCQ + l * 2 + 2],
                             rstq[0:64, 0:W], ALU.mult, ALU.mult, [pq1, rstq, self.pv], [cqn])
                if seg["pb"] is not None:
                    sf = stF.next()
                    self.stt("dve", sf[:, 0:W], pk[:, 0:W], self.pvs(PV_GCKV, l, 1, 0), rstk[:, 0:W], ALU.mult, ALU.mult,
                             [pk, rstk, self.pv], [sf])
                    self.cp("act", ckvn[:, 0:W], sf[:, 0:W], [sf], [ckvn])
                    state_out(sf, W, self.o_ckv[seg["pb"], l, t0:t0 + W, :].rearrange("(b p) f -> p b f", p=128), 0, 128)
                else:
                    self.stt("dve", ckvn[:, 0:W], pk[:, 0:W], self.pvs(PV_GCKV, l, 1, 0), rstk[:, 0:W], ALU.mult, ALU.mult,
                             [pk, rstk, self.pv], [ckvn])
                for kv in range(2):
                    pq = proj(C_KD0 + kv * 128, 128, hT, W)
                    if seg["rope"]:
                        pp = proj(C_KD0P + kv * 128, 128, hT, W)
                        rope(pq, pp, rA, slice(0, 128), W, KAt[:, kv, 0:W], KAt)
                    else:
                        self.cp("act", KAt[:, kv, 0:W], pq[:, 0:W], [pq], [KAt])
                self.st(self.KA[:, :, kcol:kcol + W].rearrange("c p t -> p c t"), KAt[:, :, 0:W], KAt)
                if seg["pb"] is not None:
                    pq = proj(C_KAN, 128, hT, W)
                    sf = stF.next()
                    self.cp("act", sf[:, 0:W], pq[:, 0:W], [pq], [sf])
                    state_out(sf, W, self.o_wk[seg["pb"], l, t0:t0 + W, :].rearrange("(b p) f -> p b f", p=128), 0, 128)
                if ti + 1 < len(tl):
                    hT_next = pre_norm(ti + 1, xt_next, presquared=True)
                    if ti + 2 < len(tl):
                        xt_next = pre_x(ti + 2)
                ps = self.psr.next()
                for b in range(nb):
                    self.mm(ps[:, b * 128:(b + 1) * 128],
                            [(hT[:, k, b * 128:(b + 1) * 128], w1[:, k, C_VA:C_VA + 128]) for k in range(8)], ps, [w1, hT])
                psv = ps[:, 0:W].rearrange("p (b f) -> p b f", f=128)
                for s_ in range(4):
                    kv, off = s_ // 2, (0 if s_ % 2 == 0 else 64)
                    self.cp("act" if s_ % 2 else "dve", VAt[:, 0:nb, s_, off:off + 64], psv[:, :, kv * 64:(kv + 1) * 64], [ps], [VAt])
                if seg["pb"] is not None:
                    self.cp("dve", vaF[:, 0:nb, :], psv, [ps], [vaF])
                    self.st(self.o_wv[seg["pb"], l, t0:t0 + W, :].rearrange("(b p) f -> p b f", p=128), vaF[:, 0:nb, :], vaF)
                self.st(self.VA[kcol // 128:kcol // 128 + nb].rearrange("c p f -> p c f"),
                        VAt[:, 0:nb].rearrange("p b s f -> p b (s f)"), VAt)
                for f in range(2):
                    pq = proj(C_FB + f * 128, 128, hT, W)
                    self.cp("act", fbT[:, f, 0:W], pq[:, 0:W], [pq], [fbT])
                for b in range(nb):
                    ps = self.psr.next()
                    for f in range(2):
                        self.mm(ps[:, f * 256:(f + 1) * 256], [(fbT[:, f, b * 128:(b + 1) * 128], self.d64[:])], ps, [fbT, self.d64])
                    self.cp("dve" if b % 2 else "act", Zt[:, b, :], ps[:, :], [ps], [Zt])
                self.st(self.Z[col // 128:col // 128 + nb].rearrange("c p f -> p c f"), Zt[:, 0:nb, :], Zt)
                if need_q:
                    for h in range(4):
                        def uq(c0):
                            ps_ = self.psr.next()
                            self.mm(ps_[0:96, 0:W], [(wuq[:, 0, c0:c0 + 96], cqn[:, 0, 0:W]),
                                                     (wuq[0:64, 1, c0:c0 + 96], cqn[0:64, 1, 0:W])], ps_, [wuq, cqn])
                            return ps_
                        pq = uq(h * 96)
                        if seg["rope"]:
                            pp = uq(384 + h * 96)
                            rope(pq, pp, rC, slice(0, 96), W, QCt[0:96, h, 0:W], QCt)
                        else:
                            self.cp("act", QCt[0:96, h, 0:W], pq[0:96, 0:W], [pq], [QCt])
                    self.st(self.QC[:, :, col:col + W].rearrange("c p t -> p c t"), QCt[:, :, 0:W], QCt)
                self.mla_kv(ckvn, W, KCt, VCt, wukv)
                pq = proj(C_KR, 96, hT, W)
                rows = slice(64, 96)
                if seg["rope"]:
                    pp = proj(C_KRP, 96, hT, W)
                    rope(pq, pp, rC, rows, W, KCt[rows, 0, 0:W], KCt, [KCt[rows, h, 0:W] for h in range(1, 4)])
                else:
                    for h in range(4):
                        self.cp("act" if h % 2 else "dve", KCt[rows, h, 0:W], pq[rows, 0:W], [pq], [KCt])
                    self.cp("act", krF[rows, 0:W], pq[rows, 0:W], [pq], [krF])
                    state_out(krF, W, self.o_kr[seg["pb"], l, t0:t0 + W, :].rearrange("(b p) f -> p b f", p=128), 64, 96)
                self.st(self.KC[:, :, kcol:kcol + W].rearrange("c p t -> p c t"), KCt[:, :, 0:W], KCt)
                self.st(self.VC[kcol // 128:kcol // 128 + nb].rearrange("c p f -> p c f"),
                        VCt[:, 0:nb].rearrange("p b s f -> p b (s f)"), VCt)
        kc0 = NS
        W = PAST
        nb = 2
        ci = ctxin.next()
        self.ld(ci[:], self.cwk[l].rearrange("(b p) f -> p b f", p=128), ci)
        ps = self.psr.next()
        for b in range(nb):
            self.transpose(ps[:, b * 128:(b + 1) * 128], ci[:, b, :], ps, [ci])
        for kv in range(2):
            src = slice(kv * 64, kv * 64 + 64)
            self.cp("dve", KAt[0:64, kv, 0:W], ps[src, 0:W], [ps], [KAt])
            self.cp("act", KAt[64:128, kv, 0:W], ps[src, 0:W], [ps], [KAt])
        self.st(self.KA[:, :, kc0:kc0 + W].rearrange("c p t -> p c t"), KAt[:, :, 0:W], KAt)
        ci = ctxin.next()
        self.ld(ci[:], self.cwv[l].rearrange("(b p) f -> p b f", p=128), ci)
        for s_ in range(4):
            kv, off = s_ // 2, (0 if s_ % 2 == 0 else 64)
            self.cp("act" if s_ % 2 else "dve", VAt[:, 0:nb, s_, off:off + 64], ci[:, :, kv * 64:(kv + 1) * 64], [ci], [VAt])
        self.st(self.VA[kc0 // 128:kc0 // 128 + nb].rearrange("c p f -> p c f"), VAt[:, 0:nb].rearrange("p b s f -> p b (s f)"), VAt)
        ci = ctxin.next()
        self.ld(ci[:], self.cckv[l].rearrange("(b p) f -> p b f", p=128), ci)
        ps = self.psr.next()
        for b in range(nb):
            self.transpose(ps[:, b * 128:(b + 1) * 128], ci[:, b, :], ps, [ci])
        self.cp("act", ckvn[:, 0:W], ps[:, 0:W], [ps], [ckvn])
        self.mla_kv(ckvn, W, KCt, VCt, wukv)
        ci = ctxin.next()
        self.memset("pool", ci[:], 0.0, ci)
        self.ld(ci[:, :, 64:96], self.ckr[l].rearrange("(b p) f -> p b f", p=128), ci)
        ps = self.psr.next()
        for b in range(nb):
            self.transpose(ps[:, b * 128:(b + 1) * 128], ci[:, b, :], ps, [ci])
        for h in range(4):
            self.cp("act" if h % 2 else "dve", KCt[64:96, h, 0:W], ps[64:96, 0:W], [ps], [KCt])
        self.st(self.KC[:, :, kc0:kc0 + W].rearrange("c p t -> p c t"), KCt[:, :, 0:W], KCt)
        self.st(self.VC[kc0 // 128:kc0 // 128 + nb].rearrange("c p f -> p c f"), VCt[:, 0:nb].rearrange("p b s f -> p b (s f)"), VCt)

    def attn_tile(self, heads, W, scale, pr, recr, reads, LA=3):
        items = []
        for hi, hd in enumerate(heads):
            n = len(hd["keys"])
            for i, key in enumerate(hd["keys"]):
                items.append((hi, i, n, key))
        sb = {}

        def issue_s(j):
            hi, i, n, (k_ap, v_ap, m_ap) = items[j]
            S = self.pss.next()
            self.mm(S[:, 0:W], [(k_ap, heads[hi]["q"])], S, reads)
            sb[j] = S

        for j in range(min(LA, len(items))):
            issue_s(j)
        O = None
        for j, (hi, i, n, (k_ap, v_ap, m_ap)) in enumerate(items):
            hd = heads[hi]
            if i == 0:
                O = self.pso.next()
            S = sb.pop(j)
            P = pr.next()
            self.act(P[:, 0:W], S[:, 0:W], AF.Exp, [S], [P], scale=scale)
            if m_ap is not None:
                self.tt("pool", P[:, 0:W], P[:, 0:W], m_ap, ALU.mult, [P, self.mask], [P])
            if j + LA < len(items):
                issue_s(j + LA)
            self.mm(O[:, 0:W], [(v_ap, P[:, 0:W])], O, reads + [P], start=(i == 0), stop=(i == n - 1))
            if i == n - 1:
                orows, drows = (slice(0, 64), slice(64, 128)) if hd["vfirst"] else (slice(64, 128), slice(0, 64))
                rec = recr.next()
                if hd["sink"] is not None:
                    self.ts("dve", rec[orows, 0:W], O[drows, 0:W], hd["sink"](drows), None, ALU.add, None, [O, self.esink], [rec])
                    self.recip(rec[orows, 0:W], rec[orows, 0:W], [rec], [rec])
                else:
                    self.recip(rec[orows, 0:W], O[drows, 0:W], [O], [rec])
                self.tt("dve", hd["dst"](orows), O[orows, 0:W], rec[orows, 0:W], ALU.mult, [O, rec], [hd["dst_buf"]])

    def attn_tile_grp(self, heads, W, scale, pgr, recr, reads):
        items = []
        for hi, hd in enumerate(heads):
            ng = len(hd["groups"])
            for gi, grp in enumerate(hd["groups"]):
                items.append((hi, gi, ng, grp))
        slots = [2, 4, 6]
        LA = len(slots) - 1
        st = {"n": 0}
        sb = {}

        def issue_s(j):
            hi, gi, ng, grp = items[j]
            assert len(grp["keys"]) <= 2
            b = slots[st["n"] % len(slots)]
            st["n"] += 1
            for off, (k_ap, v_ap) in enumerate(grp["keys"]):
                S = self.psum[b + off]
                pairs = [(k_ap, heads[hi]["q"])]
                if grp["mask"] is not None:
                    pairs.append((self.identb[:], grp["mask"][off]))
                self.mm(S[:, 0:W], pairs, S, reads + [self.identb, self.mask])
            sb[j] = b

        for j in range(min(LA, len(items))):
            issue_s(j)
        O = None
        for j, (hi, gi, ng, grp) in enumerate(items):
            hd = heads[hi]
            if gi == 0:
                O = self.pso.next()
            b = sb.pop(j)
            n = len(grp["keys"])
            P = pgr.next()
            self.act(P[:, 0:n, 0:W], self.psall[:, b:b + n, 0:W], AF.Exp, [self.psum[b + i] for i in range(n)], [P], scale=scale)
            if j + LA < len(items):
                issue_s(j + LA)
            self.mm(O[:, 0:W], [(v_ap, P[:, i, 0:W]) for i, (k_ap, v_ap) in enumerate(grp["keys"])], O, reads + [P],
                    start=(gi == 0), stop=(gi == ng - 1))
            if gi == ng - 1:
                orows, drows = (slice(0, 64), slice(64, 128)) if hd["vfirst"] else (slice(64, 128), slice(0, 64))
                rec = recr.next()
                if hd["sink"] is not None:
                    self.ts("dve", rec[orows, 0:W], O[drows, 0:W], hd["sink"](drows), None, ALU.add, None, [O, self.esink], [rec])
                else:
                    self.cp("dve", rec[orows, 0:W], O[drows, 0:W], [O], [rec])
                self.recip_pool(rec[orows, 0:W], rec, orows, W)
                self.tt("dve", hd["dst"](orows), O[orows, 0:W], rec[orows, 0:W], ALU.mult, [O, rec], [hd["dst_buf"]])

    def attn_tile_pair(self, heads, W, scale, ppr, recr, reads, LA=2):
        items = []
        for hi, hd in enumerate(heads):
            ks = hd["keys"]
            assert len(ks) % 2 == 0
            for i in range(0, len(ks), 2):
                items.append((hi, i // 2, len(ks) // 2, ks[i], ks[i + 1]))
        pairs = [2, 4, 6]
        st = {"n": 0}
        sb = {}

        def issue_s(j):
            hi, i, n, ka, kb = items[j]
            b = pairs[st["n"] % 3]
            st["n"] += 1
            for off, key in ((0, ka), (1, kb)):
                S = self.psum[b + off]
                self.mm(S[:, 0:W], [(key[0], heads[hi]["q"])], S, reads)
            sb[j] = b

        for j in range(min(LA, len(items))):
            issue_s(j)
        O = None
        for j, (hi, i, n, ka, kb) in enumerate(items):
            hd = heads[hi]
            if i == 0:
                O = self.pso.next()
            b = sb.pop(j)
            P = ppr.next()
            sv = BankView(self.psall, b, 2)
            self.act(P[:, :, 0:W], sv[:, 0:W], AF.Exp, [self.psum[b], self.psum[b + 1]], [P], scale=scale)
            if j + LA < len(items):
                issue_s(j + LA)
            self.mm(O[:, 0:W], [(ka[1], P[:, 0, 0:W]), (kb[1], P[:, 1, 0:W])], O, reads + [P], start=(i == 0), stop=(i == n - 1))
            if i == n - 1:
                orows, drows = (slice(0, 64), slice(64, 128)) if hd["vfirst"] else (slice(64, 128), slice(0, 64))
                rec = recr.next()
                self.recip(rec[orows, 0:W], O[drows, 0:W], [O], [rec])
                self.tt("dve", hd["dst"](orows), O[orows, 0:W], rec[orows, 0:W], ALU.mult, [O, rec], [hd["dst_buf"]])

    def phase_ba(self, l):
        self.phase_begin()
        KAz = [self.sb([128, 2, NS + PAST], BF16), self.sb([128, 2, NS + PAST], BF16)]
        self.memset("pool", KAz[0][64:128, :, :], 0.0, KAz[0])
        self.memset("dve", KAz[1][0:64, :, :], 0.0, KAz[1])
        VAs = self.sb([128, (NS + PAST) // 128, 4, 128], BF16)
        qr = self.sb([128, 4, 512], BF16, n=2)
        mr = self.sb([128, 4, 512], BF16, n=2)
        pgr = self.sb([128, 2, 512], BF16, n=6)
        recr = self.sb([128, 512], F32, n=4)
        for seg in SEGS:
            nk = seg["nk"]
            nch = nk // 128
            self.ld(KAz[0][0:64, :, 0:nk], self.KA[:, 0:64, seg["koff"]:seg["koff"] + nk].rearrange("c p t -> p c t"), KAz[0])
            self.ld(KAz[1][64:128, :, 0:nk], self.KA[:, 64:128, seg["koff"]:seg["koff"] + nk].rearrange("c p t -> p c t"), KAz[1])
            self.ld(VAs[:, 0:nch].rearrange("p c s f -> p c (s f)"),
                    self.VA[seg["koff"] // 128:seg["koff"] // 128 + nch].rearrange("c p f -> p c f"), VAs)
            qts = qtiles(seg, l)

            def ldq(i_):
                t0_, W_ = qts[i_]
                c_ = seg["off"] + t0_
                q_ = qr.next()
                self.ld(q_[:, :, 0:W_], self.QA[:, :, c_:c_ + W_].rearrange("c p t -> p c t"), q_)
                return q_

            qt_next = ldq(0)
            for ti_, (t0, W) in enumerate(qts):
                col = seg["off"] + t0
                qt = qt_next
                if ti_ + 1 < len(qts):
                    qt_next = ldq(ti_ + 1)
                mt = mr.next()
                n0 = t0 // 128
                heads = []
                for c in range(4):
                    kv = c // 2
                    for half in range(2):
                        h = 2 * c + half
                        rows = slice(half * 64, half * 64 + 64)
                        def kv_(jb):
                            return (KAz[half][:, kv, jb * 128:(jb + 1) * 128], VAs[:, jb, kv * 2 + half, :])
                        groups = []
                        if seg["rope"]:
                            band = [jb for jb in range(n0 - 1, n0 + 5) if 0 <= jb < NS // 128]
                            for i0 in range(0, len(band), 2):
                                sub = band[i0:i0 + 2]
                                r0 = sub[0] - n0 + 1
                                groups.append(dict(keys=[kv_(jb) for jb in sub],
                                                   mask=[self.mask[:, r0 + i_, 0:W] for i_ in range(len(sub))]))
                            groups.append(dict(keys=[kv_(jb) for jb in range(NS // 128, NS // 128 + 2)], mask=None))
                        else:
                            groups.append(dict(keys=[kv_(jb) for jb in range(nch)], mask=None))
                        heads.append(dict(q=qt[:, c, 0:W], groups=groups, vfirst=(half == 0),
                                          sink=(lambda dr, h=h: self.esink[dr, l * 8 + h:l * 8 + h + 1]),
                                          dst=(lambda orr, c=c: mt[orr, c, 0:W]), dst_buf=mt))
                self.attn_tile_grp(heads, W, 0.125, pgr, recr, [KAz[0], KAz[1], VAs, qt])
                self.st(self.MIX[0:4, :, col:col + W].rearrange("c p t -> p c t"), mt[:, :, 0:W], mt)
                if l == 0 and seg["pb"] is None and t0 == 0:
                    pe_ = self.fw.eng["pe"]
                    self.fw._wait(self.fw.eng["pool"], pe_.sem, pe_.count)
                    self.cast_bg(0)

    def phase_bc(self, l):
        self.phase_begin()
        KCs = self.sb([128, 4, NS + PAST], BF16)
        VCs = self.sb([128, (NS + PAST) // 128, 4, 128], BF16)
        qr = self.sb([128, 4, 512], BF16, n=2)
        mr = self.sb([128, 2, 512], BF16, n=2)
        pgr = self.sb([128, 2, 512], BF16, n=6)
        recr = self.sb([128, 512], F32, n=4)
        sc = 96.0 ** -0.5
        for seg in SEGS:
            nk = seg["nk"]
            nch = nk // 128
            self.ld(KCs[:, :, 0:nk], self.KC[:, :, seg["koff"]:seg["koff"] + nk].rearrange("c p t -> p c t"), KCs)
            self.ld(VCs[:, 0:nch].rearrange("p c s f -> p c (s f)"),
                    self.VC[seg["koff"] // 128:seg["koff"] // 128 + nch].rearrange("c p f -> p c f"), VCs)
            qts = qtiles(seg, l)

            def ldq(i_):
                t0_, W_ = qts[i_]
                c_ = seg["off"] + t0_
                q_ = qr.next()
                self.ld(q_[:, :, 0:W_], self.QC[:, :, c_:c_ + W_].rearrange("c p t -> p c t"), q_)
                return q_

            qt_next = ldq(0)
            for ti_, (t0, W) in enumerate(qts):
                col = seg["off"] + t0
                qt = qt_next
                if ti_ + 1 < len(qts):
                    qt_next = ldq(ti_ + 1)
                mt = mr.next()
                heads = []
                for h in range(4):
                    keys = [(KCs[0:96, h, jb * 128:(jb + 1) * 128], VCs[:, jb, h, :]) for jb in range(nch)]
                    groups = [dict(keys=keys[i0:i0 + 2], mask=None) for i0 in range(0, nch, 2)]
                    heads.append(dict(q=qt[0:96, h, 0:W], groups=groups, vfirst=(h % 2 == 0), sink=None,
                                      dst=(lambda orr, h=h: mt[orr, h // 2, 0:W]), dst_buf=mt))
                self.attn_tile_grp(heads, W, sc, pgr, recr, [KCs, VCs, qt])
                self.st(self.MIX[6:8, :, col:col + W].rearrange("c p t -> p c t"), mt[:, :, 0:W], mt)
                if l == 0 and seg["pb"] is None and t0 == 0:
                    pe_ = self.fw.eng["pe"]
                    self.fw._wait(self.fw.eng["pool"], pe_.sem, pe_.count)
                    self.cast_bg(1)
                    self.cast_bg(2)

    def phase_bf(self, l):
        self.phase_begin()
        Zs = self.sb([128, NS // 128, 512], BF16)
        cr = self.sb([128, 8, 512], BF16, n=3)
        sr = self.sb([128, 8, 512], BF16, n=3)
        mr = self.sb([128, 2, 512], BF16, n=2)
        for seg in SEGS:
            n = seg["n"]
            nch = n // 128
            self.ld(Zs[:, 0:nch, :], self.Z[seg["off"] // 128:seg["off"] // 128 + nch].rearrange("c p f -> p c f"), Zs)
            scale = (n * 64.0) ** -0.5
            for t0, W in qtiles(seg, l):
                col = seg["off"] + t0
                mt = mr.next()
                acc = [self.psr.next(), self.psr.next()]
                if seg["rope"]:
                    for gp in range(4):
                        cb, sbb = cr.next(), sr.next()
                        self.ld(cb[:], self.dfts[0, t0 // 512, :, gp * 8:(gp + 1) * 8, :], cb)
                        self.ld(sbb[:], self.dfts[1, t0 // 512, :, gp * 8:(gp + 1) * 8, :], sbb)
                        for f in range(2):
                            pairs = []
                            for j in range(8):
                                nc_ = gp * 8 + j
                                pairs.append((Zs[:, nc_, f * 256:f * 256 + 128], cb[:, j, 0:W]))
                                pairs.append((Zs[:, nc_, f * 256 + 128:f * 256 + 256], sbb[:, j, 0:W]))
                            self.mm(acc[f][:, 0:W], pairs, acc[f], [Zs, cb, sbb], start=(gp == 0), stop=(gp == 3))
                else:
                    for f in range(2):
                        pairs = []
                        for j in range(nch):
                            pairs.append((Zs[:, j, f * 256:f * 256 + 128], self.dp[:, 0, j, 0:W]))
                            pairs.append((Zs[:, j, f * 256 + 128:f * 256 + 256], self.dp[:, 1, j, 0:W]))
                        self.mm(acc[f][:, 0:W], pairs, acc[f], [Zs, self.dp])
                for f in range(2):
                    self.act(mt[:, f, 0:W], acc[f][:, 0:W], AF.Identity, [acc[f]], [mt], scale=scale)
                self.st(self.MIX[4:6, :, col:col + W].rearrange("c p t -> p c t"), mt[:, :, 0:W], mt)

    def phase_bo(self, l):
        self.phase_begin()
        self.fw.wait_bg("sp", f"out{l}")
        wo = self.sb([128, 8, D], BF16)
        for k in range(8):
            self.ld(wo[:, k, :], self.woutb[l, :, k, :], wo)
        mr = self.sb([128, 8, 512], BF16, n=2)
        xr = self.sb([128, 8, 512], F32, n=2)
        xo = self.sb([128, 8, 512], F32, n=2)
        for seg in SEGS:
            g = seg["grp"]
            qts = qtiles(seg, l)

            def ldmx(i_):
                t0_, W_ = qts[i_]
                c_ = seg["off"] + t0_
                m_ = mr.next()
                self.ld(m_[:, :, 0:W_], self.MIX[:, :, c_:c_ + W_].rearrange("c p t -> p c t"), m_)
                x_ = xr.next()
                self.ld(x_[:, :, 0:W_], self.X0[:, :, c_:c_ + W_].rearrange("k p t -> p k t"), x_)
                return m_, x_

            nxt = ldmx(0)
            for ti_, (t0, W) in enumerate(qts):
                col = seg["off"] + t0
                mt, xt = nxt
                if ti_ + 1 < len(qts):
                    nxt = ldmx(ti_ + 1)
                xn = xo.next()
                for m in range(8):
                    ps = self.psr.next()
                    self.mm(ps[:, 0:W], [(wo[:, k, m * 128:(m + 1) * 128], mt[:, k, 0:W]) for k in range(8)], ps, [wo, mt])
                    self.stt("dve", xn[:, m, 0:W], ps[:, 0:W], self.mod[:, l, g, 16 + m:17 + m], xt[:, m, 0:W],
                             ALU.mult, ALU.add, [ps, xt, self.mod], [xn])
                self.st(self.X1[:, :, col:col + W].rearrange("k p t -> p k t"), xn[:, :, 0:W], xn)

    def phase_c(self, l):
        self.phase_begin()
        self.fw.wait_bg("sp", f"ffn{l}")
        wd = self.sb([128, NFC, D], BF16)
        actb = [self.sb([128, 512], BF16) for _ in range(NFC)]
        xr = self.sb([128, 8, 512], F32)
        xsb = self.sb([128, 8, 512], F32)
        sq = self.sb([128, 8, 512], BF16)
        h2r = self.sb([128, 8, 512], BF16, n=2)
        tmpr = self.sb([128, 512], F32, n=2)
        rstd = self.sb([128, 512], F32)
        wgrp = self.sb([128, 2, 8, 256], BF16, n=4)
        cvr = self.sb([128, 512], F32, n=8)
        sgr = self.sb([128, 512], F32, n=3)
        xor_ = self.sb([128, 512], F32, n=2)
        halos = [self.sb([128, 2, NFC, 2], F32), self.sb([128, 2, NFC, 2], F32)]
        hterm = self.sb([128, 2, NFC, 2], F32)
        ht2 = self.sb([128, 2, NFC], F32)
        fin = self.sb([128, 2, NFC], F32)
        fin2 = self.sb([128, 2, NFC], F32)
        actf = self.sb([128, NFC, 2], BF16)
        xf = self.sb([128, 8, 2], F32)
        xfo = self.sb([128, 8, 2], F32)
        NG = NFC // 2
        PF = 3
        cwls = [self.pv[:, base + l * 132:base + (l + 1) * 132].rearrange("p (t a c) -> p t a c", t=3, a=2)
                for base in (PV_CW, PV_CWS)]
        cbl = self.pv[:, PV_CB + l * 44:PV_CB + (l + 1) * 44].rearrange("p (a c) -> p a c", a=2)
        cwbase = [PV_CW]

        def cwv(tap, ci):
            c_ = cwbase[0] + l * 132 + tap * 44 + ci
            return self.pv[:, c_:c_ + 1]

        def cbv(ci):
            c_ = PV_CB + l * 44 + ci
            return self.pv[:, c_:c_ + 1]

        tiles = []
        for seg in SEGS:
            qt_ = qtiles(seg, l)
            full = (qt_[-1][0] + qt_[-1][1] == seg["n"])
            for j, (t0_, W_) in enumerate(qt_):
                tiles.append((seg, j, len(qt_) if full else -1, W_, t0_))
        wq = []

        def tinfo(ti):
            seg, j, ntile, W, t0_ = tiles[ti]
            return seg, j, ntile, W, seg["off"] + t0_

        def load_x(ti):
            seg, j, _, W, col = tinfo(ti)
            self.ld(xr[:, :, 0:W], self.X1[:, :, col:col + W].rearrange("k p t -> p k t"), xr)

        def load_w(g_):
            wgb = wgrp.next()
            self.ld(wgb[:].rearrange("p a k j -> p (a k j)"), self.wugb[l, g_], wgb)
            wq.append(wgb)

        def do_norm(ti, presquared=False):
            seg, j, _, W, col = tinfo(ti)
            g = seg["grp"]
            h2 = h2r.next()
            self.norm_mod(xr, W, lambda k: self.s2[:, l, g, k:k + 1], lambda k: self.mod[:, l, g, 24 + k:25 + k],
                          h2, sq, rstd, tmpr, presquared=presquared)
            return h2

        def do_square(ti):
            seg, j, _, W, col = tinfo(ti)
            self.act(sq[:, :, 0:W], xr[:, :, 0:W], AF.Square, [xr], [sq])

        load_x(0)
        for g_ in range(PF):
            load_w(g_)
        for c in range(0, NFC, 2):
            self.ld(wd[:, c:c + 2, :], self.wdnb[l, :, c:c + 2, :], wd)
        h2 = do_norm(0)
        hcur = 0
        for ti in range(len(tiles)):
            seg, j, ntile, W, col = tinfo(ti)
            g = seg["grp"]
            n = seg["n"]
            cwbase[0] = PV_CWS if g == 1 else PV_CW
            cwl = cwls[g]
            if j == 0:
                self.memset("pool", halos[hcur][:], 0.0, halos[hcur])
            halo, hnew = halos[hcur], halos[1 - hcur]
            self.tt("pool", ht2[:], halo[:, :, :, 1], cwl[:, 1], ALU.mult, [halo, self.pv], [ht2])
            self.tt("pool", hterm[:, :, :, 0], halo[:, :, :, 0], cwl[:, 0], ALU.mult, [halo, self.pv], [hterm])
            self.tt("pool", hterm[:, :, :, 0], hterm[:, :, :, 0], ht2[:], ALU.add, [hterm, ht2], [hterm])
            self.tt("pool", hterm[:, :, :, 1], halo[:, :, :, 1], cwl[:, 0], ALU.mult, [halo, self.pv], [hterm])
            if ti + 1 < len(tiles):
                load_x(ti + 1)
            h2n = None
            pend = []

            def gate(c_, cvs_, eng_="pool"):
                sg = sgr.next()
                self.act(sg[:, 0:W], cvs_[1][:, 0:W], AF.Silu, [cvs_[1]], [sg])
                self.tt(eng_, actb[c_][:, 0:W], sg[:, 0:W], cvs_[0][:, 0:W], ALU.mult, [sg, cvs_[0]], [actb[c_]])

            for gi in range(NG):
                if gi + PF < NG:
                    load_w(gi + PF)
                wgb = wq.pop(0)
                for cc in range(2):
                    c = gi * 2 + cc
                    pss_, cvs = [], []
                    for ag in range(2):
                        ps = self.psr.next()
                        self.mm(ps[:, 0:W], [(wgb[:, ag, k, cc * 128:(cc + 1) * 128], h2[:, k, 0:W]) for k in range(8)], ps, [wgb, h2])
                        pss_.append(ps)
                    for ag in range(2):
                        ci = ag * NFC + c
                        cv = cvr.next()
                        self.act(cv[:, 0:W], pss_[ag][:, 0:W], AF.Identity, [pss_[ag], self.pv], [cv], bias=cbv(ci), scale=cwv(2, ci))
                        cvs.append(cv)
                    for ag in range(2):
                        self.cp("act", hnew[:, ag, c, :], pss_[ag][:, W - 2:W], [pss_[ag]], [hnew])
                    for ag in range(2):
                        ci = ag * NFC + c
                        self.stt("dve", cvs[ag][:, 1:W], pss_[ag][:, 0:W - 1], cwv(1, ci), cvs[ag][:, 1:W], ALU.mult, ALU.add,
                                 [pss_[ag], cvs[ag], self.pv], [cvs[ag]])
                    for ag in range(2):
                        ci = ag * NFC + c
                        self.stt("dve", cvs[ag][:, 2:W], pss_[ag][:, 0:W - 2], cwv(0, ci), cvs[ag][:, 2:W], ALU.mult, ALU.add,
                                 [pss_[ag], cvs[ag], self.pv], [cvs[ag]])
                    for ag in range(2):
                        self.tt("pool", cvs[ag][:, 0:2], cvs[ag][:, 0:2], hterm[:, ag, c, :], ALU.add, [cvs[ag], hterm], [cvs[ag]])
                    pend.append((c, cvs))
                    if len(pend) > 2:
                        gate(*pend.pop(0))
                if gi == 3 and ti + 1 < len(tiles):
                    do_square(ti + 1)
                if gi == 7 and ti + 1 < len(tiles):
                    h2n = do_norm(ti + 1, presquared=True)
            while pend:
                gate(*pend.pop(0), eng_="dve")
            lo = 1 if j == 0 else 0
            self.ld(xsb[:, :, lo:W], self.X1[:, :, col - 1 + lo:col - 1 + W].rearrange("k p t -> p k t"), xsb)
            if ti + 1 < len(tiles):
                for g_ in range(PF):
                    load_w(g_)
            for m in range(8):
                ps = self.psr.next()
                for c in range(NFC):
                    self.mm(ps[:, 0:W], [(wd[:, c, m * 128:(m + 1) * 128], actb[c][:, 0:W])], ps, [wd, actb[c]],
                            start=(c == 0), stop=(c == NFC - 1), signal=(c == NFC - 1))
                xo = xor_.next()
                self.stt("dve", xo[:, lo:W], ps[:, lo:W], self.mod[:, l, g, 40 + m:41 + m], xsb[:, m, lo:W],
                         ALU.mult, ALU.add, [ps, xsb, self.mod], [xo])
                self.st(self.X0[m, :, col - 1 + lo:col - 1 + W], xo[:, lo:W], xo)
            hcur = 1 - hcur
            if j == ntile - 1:
                halo = halos[hcur]
                cend = seg["off"] + n - 1
                self.tt("dve", fin[:], halo[:, :, :, 0], cwl[:, 0], ALU.mult, [halo, self.pv], [fin])
                self.tt("dve", fin2[:], halo[:, :, :, 1], cwl[:, 1], ALU.mult, [halo, self.pv], [fin2])
                self.tt("dve", fin[:], fin[:], fin2[:], ALU.add, [fin, fin2], [fin])
                self.tt("dve", fin[:], fin[:], cbl, ALU.add, [fin, self.pv], [fin])
                self.act(fin2[:, 1, :], fin[:, 1, :], AF.Silu, [fin], [fin2])
                self.memset("pool", actf[:], 0.0, actf)
                self.tt("dve", actf[:, :, 0], fin2[:, 1, :], fin[:, 0, :], ALU.mult, [fin, fin2], [actf])
                self.ld(xf[:, :, 0:1], self.X1[:, :, cend:cend + 1].rearrange("k p t -> p k t"), xf)
                for m in range(8):
                    ps = self.psr.next()
                    self.mm(ps[:, 0:2], [(wd[:, c, m * 128:(m + 1) * 128], actf[:, c, :]) for c in range(NFC)], ps, [wd, actf])
                    self.stt("dve", xfo[:, m, 0:1], ps[:, 0:1], self.mod[:, l, g, 40 + m:41 + m], xf[:, m, 0:1],
                             ALU.mult, ALU.add, [ps, xf, self.mod], [xfo])
                self.st(self.X0[:, :, cend:cend + 1].rearrange("k p t -> p k t"), xfo[:, :, 0:1], xfo)
            h2 = h2n

    def epilogue(self):
        self.phase_begin()
        xr = self.sb([128, 8, 512], F32, n=2)
        sq = self.sb([128, 8, 512], BF16)
        yT = self.sb([128, 8, 512], F32, n=2)
        rstd = self.sb([128, 512], F32)
        yo = self.sb([128, 4, D], F32, n=2)
        tl = []
        for seg in SEGS:
            nout = NQH if seg["pb"] is None else seg["n"]
            for t0 in range(0, nout, 512):
                tl.append((seg, t0, min(512, nout - t0)))

        def pre_x(i):
            seg_, t0_, W_ = tl[i]
            xt_ = xr.next()
            c_ = seg_["off"] + t0_
            self.ld(xt_[:, :, 0:W_], self.X0[:, :, c_:c_ + W_].rearrange("k p t -> p k t"), xt_)
            return xt_

        xt_next = pre_x(0)
        for ti, (seg, t0, W) in enumerate(tl):
            dst = self.y_s if seg["pb"] is None else self.y_p[seg["pb"]]
            nb = W // 128
            xt = xt_next
            if ti + 1 < len(tl):
                xt_next = pre_x(ti + 1)
            self.act(sq[:, :, 0:W], xt[:, :, 0:W], AF.Square, [xt], [sq])
            ss = self.psr.next()
            self.mm(ss[:, 0:W], [(self.ones[:], sq[:, k, 0:W]) for k in range(8)], ss, [self.ones, sq])
            self._obuf = rstd
            self.rsqrt_ps(rstd[:, 0:W], ss[:, 0:W], ss, 1.0 / D)
            y = yT.next()
            for k in range(8):
                self.stt("dve", y[:, k, 0:W], xt[:, k, 0:W], self.pv[:, PV_GFIN + k:PV_GFIN + k + 1], rstd[:, 0:W],
                         ALU.mult, ALU.mult, [xt, rstd, self.pv], [y])
            yy = yo.next()
            for b_ in range(nb):
                for k4 in range(2):
                    ps = self.psr.next()
                    for kk in range(4):
                        k = k4 * 4 + kk
                        self.transpose(ps[:, kk * 128:(kk + 1) * 128], y[:, k, b_ * 128:(b_ + 1) * 128], ps, [y])
                    self.cp("act" if k4 else "dve", yy[:, b_, k4 * 512:(k4 + 1) * 512], ps[:, :], [ps], [yy])
            self.st(dst[t0:t0 + W, :].rearrange("(b p) f -> p b f", p=128), yy[:, 0:nb, :], yy)

    def build(self):
        self.setup_persistent()
        self.phase_begin()
        self.prologue()
        self.prologue2()
        stages = []
        for l in range(L):
            stages += [("a", l), ("ba", l), ("bc", l), ("bf", l), ("bo", l), ("c", l)]
        for nm, l in stages:
            if self.stop is not None and (nm, l) == self.stop:
                break
            getattr(self, "phase_" + nm)(l)
        else:
            self.epilogue()
        self.fw.finish()
        return self.nc


PV_GMIX = 0
PV_GFFN = PV_GMIX + L * 8
PV_GFIN = PV_GFFN + L * 8
PV_BADA = PV_GFIN + 8
PV_GCQ = PV_BADA + L * 48
PV_GCKV = PV_GCQ + L * 2
PV_SINK = PV_GCKV + L
PV_CW = PV_SINK + L * 8
PV_CB = PV_CW + L * 132
PV_C = PV_CB + L * 44
PV_CWS = PV_C + 16
PV_N = PV_CWS + L * 132


def _perm_partner(dim):
    q = dim // 4
    idx = np.arange(dim).reshape(2, 2, q)
    return idx[:, ::-1, :].reshape(dim)


def _rope_tables(dim, rows_used, row_off, n):
    q = dim // 4
    inv = (10000.0 ** (-np.arange(q, dtype=np.float32) / q)).astype(np.float32)
    t = np.arange(n)
    r = (t // 64).astype(np.float32)
    c = (t % 64).astype(np.float32)
    tab = np.zeros((2, 128, n), np.float32)
    tab[0] = 1.0
    for p in range(rows_used):
        d = p % dim
        ax = r if d < dim // 2 else c
        dd = d % (dim // 2)
        ang = (ax * inv[dd % q]).astype(np.float32)
        tab[0, row_off + p] = np.cos(ang)
        tab[1, row_off + p] = np.sin(ang) * (-1.0 if dd < q else 1.0)
    return tab


_CONST = {}


def _consts():
    if _CONST:
        return _CONST
    bf = ml_dtypes.bfloat16
    _CONST["ropeA"] = _rope_tables(64, 128, 0, NS)
    _CONST["ropeC"] = _rope_tables(32, 32, 64, NS)
    b = np.arange(128)[:, None, None]
    rel = np.arange(6)[None, :, None]
    qi = np.arange(512)[None, None, :]
    d = qi - (rel - 1) * 128 - b
    _CONST["maskc"] = np.where(np.abs(d) <= 128, 0.0, -30000.0).astype(np.float32).astype(bf)
    j = np.arange(64)
    ang = 2 * np.pi * np.outer(j, j) / 64
    C64, S64 = np.cos(ang), np.sin(ang)
    d64 = np.zeros((128, 256), np.float64)
    for g in range(2):
        d64[g * 64:(g + 1) * 64, g * 64:(g + 1) * 64] = C64
        d64[g * 64:(g + 1) * 64, 128 + g * 64:128 + (g + 1) * 64] = -S64
    _CONST["dft64"] = d64.astype(np.float32).astype(bf)
    n = np.arange(NP)
    nk = (np.outer(n, n) % NP).astype(np.float64)
    a = 2 * np.pi * nk / NP
    dp = np.stack([np.cos(a), np.sin(a)]).reshape(2, 2, 128, NP).transpose(0, 2, 1, 3)
    _CONST["dftp"] = np.ascontiguousarray(dp).astype(np.float32).astype(bf)
    n = np.arange(NS, dtype=np.int64)
    nk = (np.outer(n, n) % NS)
    tabc = np.cos(2 * np.pi * np.arange(NS) / NS).astype(np.float32).astype(bf)
    tabs = np.sin(2 * np.pi * np.arange(NS) / NS).astype(np.float32).astype(bf)
    out = np.empty((2, 8, 128, 32, 512), bf)
    for i, tab in enumerate((tabc, tabs)):
        m = tab[nk]
        out[i] = m.reshape(32, 128, 8, 512).transpose(2, 1, 0, 3)
    _CONST["dfts"] = out
    _CONST["ropeA_r"] = np.ascontiguousarray(_CONST["ropeA"][:, :, ::-1])
    _CONST["ropeC_r"] = np.ascontiguousarray(_CONST["ropeC"][:, :, ::-1])
    n1 = np.arange(1, NS + 1, dtype=np.int64)
    nk1 = (np.outer(n1, n1) % NS)
    outr = np.empty((2, 8, 128, 32, 512), bf)
    for i, tab in enumerate((tabc, tabs)):
        m = tab[nk1]
        outr[i] = m.reshape(32, 128, 8, 512).transpose(2, 1, 0, 3)
    _CONST["dfts_r"] = outr
    return _CONST


def _layout_weights(w_in, w_uq, w_ukv, w_out, w_ug, w_down, w_ada):
    f = np.float32
    pa = _perm_partner(64)
    pc = _perm_partner(32)
    w1 = np.empty((L, D, C1), f)
    qa = w_in[:, :, 0:512]
    w1[:, :, C_QA:C_QA + 512] = qa
    w1[:, :, C_QAP:C_QAP + 512] = qa.reshape(L, D, 8, 64)[:, :, :, pa].reshape(L, D, 512)
    ka = w_in[:, :, 512:640].reshape(L, D, 2, 64)
    kap = ka[:, :, :, pa]
    for kv in range(2):
        w1[:, :, C_KD0 + kv * 128:C_KD0 + kv * 128 + 64] = ka[:, :, kv]
        w1[:, :, C_KD0 + kv * 128 + 64:C_KD0 + kv * 128 + 128] = ka[:, :, kv]
        w1[:, :, C_KD0P + kv * 128:C_KD0P + kv * 128 + 64] = kap[:, :, kv]
        w1[:, :, C_KD0P + kv * 128 + 64:C_KD0P + kv * 128 + 128] = kap[:, :, kv]
    w1[:, :, C_KAN:C_KAN + 128] = w_in[:, :, 512:640]
    w1[:, :, C_VA:C_VA + 128] = w_in[:, :, 640:768]
    w1[:, :, C_FB:C_FB + 256] = w_in[:, :, 768:1024]
    w1[:, :, C_CQ:C_CQ + 192] = w_in[:, :, 1024:1216]
    w1[:, :, C_CKV:C_CKV + 128] = w_in[:, :, 1216:1344]
    w1[:, :, C_KR:C_KR + 64] = w_in[:, :, 1280:1344]
    w1[:, :, C_KR + 64:C_KR + 96] = w_in[:, :, 1344:1376]
    w1[:, :, C_KRP:C_KRP + 64] = w_in[:, :, 1280:1344]
    w1[:, :, C_KRP + 64:C_KRP + 96] = w_in[:, :, 1344:1376][:, :, pc]
    w1 = np.ascontiguousarray(w1.reshape(L, 8, 128, C1).transpose(0, 2, 1, 3))
    uq = np.zeros((L, 256, 768), f)
    uq[:, 0:192, 0:384] = w_uq
    uqp = w_uq.reshape(L, 192, 4, 96).copy()
    uqp[:, :, :, 64:96] = uqp[:, :, :, 64:96][:, :, :, pc]
    uq[:, 0:192, 384:768] = uqp.reshape(L, 192, 384)
    uq = np.ascontiguousarray(uq.reshape(L, 2, 128, 768).transpose(0, 2, 1, 3))
    kvw = w_ukv.reshape(L, 128, 4, 128)
    ukv = np.ascontiguousarray(np.concatenate([kvw[:, :, :, 0:64].reshape(L, 128, 256),
                                               kvw[:, :, :, 64:128].reshape(L, 128, 256)], axis=2))
    wo = np.ascontiguousarray(w_out.reshape(L, 8, 128, D).transpose(0, 2, 1, 3))
    wug = np.ascontiguousarray(w_ug.reshape(L, 8, 128, 2, 11, 256).transpose(0, 4, 2, 3, 1, 5)).reshape(L, 11, 128, 2 * 8 * 256)
    wdn = np.ascontiguousarray(w_down.reshape(L, NFC, 128, D).transpose(0, 2, 1, 3))
    wad = np.ascontiguousarray(w_ada.reshape(L, 8, 128, 6 * D).transpose(0, 2, 1, 3))
    return dict(w1=w1, wuq=uq, wukv=ukv, wout=wo, wug=wug, wdn=wdn, wada=wad)


def _pvec(c_vec, c_ctx, b_ada, g_mix, sink, g_cq, g_ckv, g_ffn, conv_w, conv_b, g_final, rev):
    pv = np.zeros((128, PV_N), np.float32)
    pv[:, PV_GMIX:PV_GMIX + L * 8] = g_mix.reshape(L, 8, 128).transpose(2, 0, 1).reshape(128, L * 8)
    pv[:, PV_GFFN:PV_GFFN + L * 8] = g_ffn.reshape(L, 8, 128).transpose(2, 0, 1).reshape(128, L * 8)
    pv[:, PV_GFIN:PV_GFIN + 8] = g_final.reshape(8, 128).T
    pv[:, PV_BADA:PV_BADA + L * 48] = b_ada.reshape(L, 48, 128).transpose(2, 0, 1).reshape(128, L * 48)
    gq = np.zeros((L, 256), np.float32)
    gq[:, 0:192] = g_cq
    pv[:, PV_GCQ:PV_GCQ + L * 2] = gq.reshape(L, 2, 128).transpose(2, 0, 1).reshape(128, L * 2)
    pv[:, PV_GCKV:PV_GCKV + L] = g_ckv.T
    pv[:, PV_SINK:PV_SINK + L * 8] = sink.reshape(1, L * 8)
    pv[:, PV_CW:PV_CW + L * 132] = conv_w.reshape(L, 3, 44, 128).transpose(3, 0, 1, 2).reshape(128, L * 132)
    cws = conv_w[:, ::-1, :] if rev else conv_w
    pv[:, PV_CWS:PV_CWS + L * 132] = cws.reshape(L, 3, 44, 128).transpose(3, 0, 1, 2).reshape(128, L * 132)
    pv[:, PV_CB:PV_CB + L * 44] = conv_b.reshape(L, 44, 128).transpose(2, 0, 1).reshape(128, L * 44)
    cc = np.stack([c_ctx.reshape(8, 128).T, c_vec.reshape(8, 128).T], axis=2)
    pv[:, PV_C:PV_C + 16] = cc.reshape(128, 16)
    return pv


_PROG = {}


def _get_prog(debug=False, stop=None):
    key = (debug, stop)
    if key not in _PROG:
        _PROG[key] = Prog(debug=debug, stop=stop).build()
    return _PROG[key]


def make_in_maps(x_prompt, x_sample, cache_win_k, cache_win_v, cache_mla_ckv, cache_mla_krope,
                 c, c_ctx, w_ada, b_ada, g_mix, w_in, sink, g_cq, w_uq, g_ckv, w_ukv, w_out,
                 g_ffn, w_ug, conv_w, conv_b, w_down, g_final, cores=range(8)):
    A = lambda a: np.ascontiguousarray(np.asarray(a), dtype=np.float32)
    ws = _layout_weights(A(w_in), A(w_uq), A(w_ukv), A(w_out), A(w_ug), A(w_down), A(w_ada))
    cst = _consts()
    x_prompt, x_sample = A(x_prompt), A(x_sample)
    cwk, cwv, cckv, ckr = A(cache_win_k), A(cache_win_v), A(cache_mla_ckv), A(cache_mla_krope)
    c, c_ctx = A(c), A(c_ctx)
    maps = []
    for i in cores:
        b = i // 2
        rev = (i % 2 == 1)
        m = dict(ws)
        for k_ in ("maskc", "dft64", "dftp"):
            m[k_] = cst[k_]
        for k_ in ("ropeA", "ropeC", "dfts"):
            m[k_] = cst[k_ + "_r"] if rev else cst[k_]
        m["x_s"] = np.ascontiguousarray(x_sample[b][::-1]) if rev else x_sample[b]
        m["x_p"] = x_prompt[2 * i:2 * i + 2]
        m["c_wk"] = cwk[b].reshape(L, PAST, 128)
        m["c_wv"] = cwv[b].reshape(L, PAST, 128)
        m["c_ckv"] = cckv[b]
        m["c_kr"] = ckr[b]
        m["pvec"] = _pvec(c[b], c_ctx, A(b_ada), A(g_mix), A(sink), A(g_cq), A(g_ckv), A(g_ffn), A(conv_w), A(conv_b), A(g_final), rev)
        maps.append(m)
    return maps


def kernel(**inputs):
    nc = _get_prog()
    maps = make_in_maps(**inputs)
    res = run_bass_kernel_spmd(nc, maps, core_ids=list(range(8))).results
    y_prompt = np.concatenate([res[i]["y_p"] for i in range(8)], axis=0).astype(np.float32)
    y_sample = np.stack([np.concatenate([res[2 * b]["y_s"], res[2 * b + 1]["y_s"][::-1]], axis=0)
                         for b in range(4)], axis=0).astype(np.float32)
    cat = lambda k: np.concatenate([res[i][k] for i in range(8)], axis=0).astype(np.float32)
    swk = cat("o_wk").reshape(16, L, NP, 2, 64)
    swv = cat("o_wv").reshape(16, L, NP, 2, 64)
    sckv = cat("o_ckv").reshape(16, L, NP, 128)
    skr = cat("o_kr").reshape(16, L, NP, 32)
    return (y_prompt, y_sample, swk, swv, sckv, skr)
```
